# Optimizing a Trainium2 kernel written in Bass

```python
import math
import jax, jax.numpy as jnp
from jax import lax
import numpy as np

D_MODEL = 1024
BATCH = 16
SEQ = 4096
DEPTH = 1

NSA_HEADS = 16
NSA_GROUPS = 4
NSA_HPG = NSA_HEADS // NSA_GROUPS
HEAD_DIM = 64
NSA_WIDTH = NSA_HEADS * HEAD_DIM
KV_WIDTH = NSA_GROUPS * HEAD_DIM
CMP_BLOCK = 32
CMP_STRIDE = 16
CMP_HIDDEN = 4 * HEAD_DIM
SEL_BLOCK = 64
SEL_TOPN = 8
WINDOW = 512
Q_BLOCK = 128
FORCE_SCORE = 1.0e4
REL_BUCKETS = 32
REL_MAX_DIST = 1024
SSM_WIDTH = 2 * D_MODEL
SSM_HEAD_DIM = 64
SSM_HEADS = SSM_WIDTH // SSM_HEAD_DIM
SSM_GROUPS = 4
SSM_HPG = SSM_HEADS // SSM_GROUPS
SSM_STATE = 128
CONV_WIDTH = 4
SSM_CHUNK = 128
CONV_DIM = SSM_WIDTH + 2 * SSM_GROUPS * SSM_STATE
NORM_EPS = 1e-6
IN_COLS = NSA_WIDTH + 6 * KV_WIDTH + 3 * NSA_HEADS + NSA_WIDTH + SSM_WIDTH + CONV_DIM + SSM_HEADS + 2 * D_MODEL

kernel_name = 'hybrid_nsa_ssd_gated_merge'


def column_offsets():
    sizes = (('q', NSA_WIDTH), ('k_cmp', KV_WIDTH), ('v_cmp', KV_WIDTH), ('k_slc', KV_WIDTH),
             ('v_slc', KV_WIDTH), ('k_swa', KV_WIDTH), ('v_swa', KV_WIDTH), ('nsa_gate', 3 * NSA_HEADS),
             ('z_nsa', NSA_WIDTH), ('z_ssm', SSM_WIDTH), ('xbc', CONV_DIM), ('dt', SSM_HEADS),
             ('merge_gate', 2 * D_MODEL))
    out, lo = {}, 0
    for name, n in sizes:
        out[name] = (lo, lo + n)
        lo += n
    return out


def rms_norm(x, w):
    xf = x.astype(jnp.float32)
    y = xf * lax.rsqrt(jnp.mean(xf * xf, axis=-1, keepdims=True) + NORM_EPS)
    return (y * w.astype(jnp.float32)).astype(x.dtype)


def masked_softmax(logits, mask):
    logits = jnp.where(mask, logits.astype(jnp.float32), -1e30)
    return jnp.where(mask, jax.nn.softmax(logits, axis=-1), 0.0)


def t5_bucket(dist):
    max_exact = REL_BUCKETS // 2
    d = jnp.maximum(dist, 0)
    df = jnp.maximum(d, 1).astype(jnp.float32)
    large = max_exact + (jnp.log(df / max_exact) / math.log(REL_MAX_DIST / max_exact)
                         * (REL_BUCKETS - max_exact)).astype(jnp.int32)
    return jnp.where(d < max_exact, d, jnp.minimum(large, REL_BUCKETS - 1))


def compress_blocks(kv, pos_emb, w1, b1, w2):
    b, s, g, dh = kv.shape
    ratio = CMP_BLOCK // CMP_STRIDE
    n_cmp = s // CMP_STRIDE - ratio + 1
    seg = kv.reshape(b, s // CMP_STRIDE, CMP_STRIDE, g, dh)
    blocks = jnp.concatenate([seg[:, r:r + n_cmp] for r in range(ratio)], axis=2)
    blocks = blocks + pos_emb[None, None, :, None, :]
    flat = jnp.moveaxis(blocks, 3, 2).reshape(b, n_cmp, g, CMP_BLOCK * dh)
    return jax.nn.silu(flat @ w1 + b1) @ w2


def nsa_mixer(q, k_cmp, v_cmp, k_slc, v_slc, k_swa, v_swa, gate_logits, rel_bias):
    b, s = q.shape[:2]
    G, R, dh = NSA_GROUPS, NSA_HPG, HEAD_DIM
    n_blk = s // Q_BLOCK
    n_cmp = k_cmp.shape[1]
    n_sel = s // SEL_BLOCK
    top_n = min(SEL_TOPN, n_sel)
    rel_bias = rel_bias.astype(jnp.float32)
    c_start = jnp.arange(n_cmp) * CMP_STRIDE
    cmp_end = c_start + CMP_BLOCK - 1
    s_start = jnp.arange(n_sel) * SEL_BLOCK
    overlap = ((c_start[:, None] < s_start[None, :] + SEL_BLOCK)
               & (c_start[:, None] + CMP_BLOCK > s_start[None, :])).astype(jnp.float32)
    ks_blocks = jnp.moveaxis(k_slc.reshape(b, n_sel, SEL_BLOCK, G, dh), 3, 1)
    vs_blocks = jnp.moveaxis(v_slc.reshape(b, n_sel, SEL_BLOCK, G, dh), 3, 1)
    k_swa_p = jnp.pad(k_swa, ((0, 0), (WINDOW, 0), (0, 0), (0, 0)))
    v_swa_p = jnp.pad(v_swa, ((0, 0), (WINDOW, 0), (0, 0), (0, 0)))
    bias_grp = rel_bias.reshape(REL_BUCKETS, G, R).transpose(1, 0, 2)
    g_ar = jnp.arange(G)
    j_ar = jnp.arange(n_sel)

    def head_bias(bucket):
        return jnp.transpose(rel_bias[bucket].reshape(*bucket.shape, G, R), (0, 2, 3, 1))

    def step(idx):
        bi, qb = idx
        q0 = qb * Q_BLOCK
        t = q0 + jnp.arange(Q_BLOCK)
        qt = lax.dynamic_slice_in_dim(q[bi], q0, Q_BLOCK, 0)
        kc, vc = k_cmp[bi], v_cmp[bi]
        lg = jnp.einsum('tgrd,ngd->tgrn', qt, kc).astype(jnp.float32) + head_bias(t5_bucket(t[:, None] - cmp_end[None, :]))
        p_cmp = masked_softmax(lg, (cmp_end[None, :] <= t[:, None])[:, None, None, :])
        o_cmp = jnp.einsum('tgrn,ngd->tgrd', p_cmp.astype(vc.dtype), vc)
        imp = jnp.einsum('tgrn,nj->tgj', p_cmp, overlap)
        cur = t // SEL_BLOCK
        valid = s_start[None, :] <= t[:, None]
        forced = (j_ar[None, :] == 0) | (j_ar[None, :] == cur[:, None]) | (j_ar[None, :] == cur[:, None] - 1)
        score = jnp.where(valid[:, None, :], imp + jnp.where(forced, FORCE_SCORE, 0.0)[:, None, :], -1.0)
        top_val, top_idx = lax.top_k(score, top_n)
        kg = ks_blocks[bi][g_ar[None, :, None], top_idx]
        vg = vs_blocks[bi][g_ar[None, :, None], top_idx]
        key_pos = top_idx[..., None] * SEL_BLOCK + jnp.arange(SEL_BLOCK)
        m_sel = (top_val >= 0.0)[..., None] & (key_pos <= t[:, None, None, None])
        bias = jnp.transpose(bias_grp[g_ar[None, :, None, None], t5_bucket(t[:, None, None, None] - key_pos)], (0, 1, 4, 2, 3))
        lg = jnp.einsum('tgrd,tgjpd->tgrjp', qt, kg).astype(jnp.float32) + bias
        n_keys = top_n * SEL_BLOCK
        p_sel = masked_softmax(lg.reshape(Q_BLOCK, G, R, n_keys), m_sel.reshape(Q_BLOCK, G, 1, n_keys))
        o_slc = jnp.einsum('tgrk,tgkd->tgrd', p_sel.astype(vg.dtype), vg.reshape(Q_BLOCK, G, n_keys, dh))
        kw = lax.dynamic_slice_in_dim(k_swa_p[bi], q0, WINDOW + Q_BLOCK, 0)
        vw = lax.dynamic_slice_in_dim(v_swa_p[bi], q0, WINDOW + Q_BLOCK, 0)
        spos = q0 - WINDOW + jnp.arange(WINDOW + Q_BLOCK)
        dist = t[:, None] - spos[None, :]
        m_win = (spos[None, :] >= 0) & (dist >= 0) & (dist < WINDOW)
        lg = jnp.einsum('tgrd,kgd->tgrk', qt, kw).astype(jnp.float32) + head_bias(t5_bucket(dist))
        p_win = masked_softmax(lg, m_win[:, None, None, :])
        o_swa = jnp.einsum('tgrk,kgd->tgrd', p_win.astype(vw.dtype), vw)
        gts = jax.nn.sigmoid(lax.dynamic_slice_in_dim(gate_logits[bi], q0, Q_BLOCK, 0).astype(jnp.float32))
        o = gts[..., 0:1] * o_cmp + gts[..., 1:2] * o_slc + gts[..., 2:3] * o_swa
        return o.reshape(Q_BLOCK, NSA_WIDTH).astype(q.dtype)

    b_idx = jnp.repeat(jnp.arange(b), n_blk)
    qb_idx = jnp.tile(jnp.arange(n_blk), b)
    out = lax.map(step, (b_idx, qb_idx))
    return out.reshape(b, s, NSA_WIDTH)


def causal_depthwise_conv(x, w, bias):
    y = lax.conv_general_dilated(x, w[:, None, :], window_strides=(1,), padding=[(CONV_WIDTH - 1, 0)],
                                 dimension_numbers=('NWC', 'WIO', 'NWC'), feature_group_count=x.shape[-1])
    return y + bias


def ssd_chunked(x, dt, a, bm, cm):
    b, s = x.shape[:2]
    L = SSM_CHUNK
    nc = s // L

    def to_chunks(t):
        return jnp.moveaxis(t.reshape(b, nc, L, *t.shape[2:]), 1, 0)

    tril = jnp.tril(jnp.ones((L, L), dtype=bool))

    def step(state, inp):
        xc, dtc, bc, cc = inp
        cs = jnp.cumsum(dtc * a, axis=1)
        csT = jnp.moveaxis(cs, 1, -1)
        diff = csT[..., :, None] - csT[..., None, :]
        decay = jnp.where(tril, jnp.exp(jnp.where(tril, diff, 0.0)), 0.0)
        cb = jnp.einsum('bign,bjgn->bgij', cc, bc)
        w = cb[:, :, None] * decay * jnp.moveaxis(dtc, 1, -1)[..., None, :]
        y_diag = jnp.einsum('bgrij,bjgrp->bigrp', w, xc)
        y_off = jnp.einsum('bign,bgrpn->bigrp', cc, state) * jnp.exp(cs)[..., None]
        total = cs[:, -1]
        to_end = jnp.exp(total[:, None] - cs) * dtc
        new_state = state * jnp.exp(total)[..., None, None] + jnp.einsum('bjgn,bjgr,bjgrp->bgrpn', bc, to_end, xc)
        return new_state, y_diag + y_off

    state0 = jnp.zeros((b, SSM_GROUPS, SSM_HPG, SSM_HEAD_DIM, SSM_STATE), jnp.float32)
    _, y = lax.scan(step, state0, (to_chunks(x), to_chunks(dt), to_chunks(bm), to_chunks(cm)))
    return jnp.moveaxis(y, 0, 1).reshape(x.shape)


def gated_group_rmsnorm(y, z, w):
    h = (y * jax.nn.silu(z.astype(jnp.float32))).reshape(*y.shape[:-1], SSM_GROUPS, -1)
    h = h * lax.rsqrt(jnp.mean(h * h, axis=-1, keepdims=True) + NORM_EPS)
    return (h.reshape(y.shape) * w.astype(jnp.float32)).astype(z.dtype)


def hybrid_layer(x, norm_w, w_in, cmp_pos_k, cmp_pos_v, cmp_k_w1, cmp_k_b1, cmp_k_w2, cmp_v_w1, cmp_v_b1,
                 cmp_v_w2, conv_w, conv_b, dt_bias, a_log, d_skip, ssm_norm_w, w_out_nsa, w_out_ssm, w_out, rel_bias):
    b, s, _ = x.shape
    G, R, dh = NSA_GROUPS, NSA_HPG, HEAD_DIM
    xn = rms_norm(x, norm_w)
    cols = column_offsets()

    def proj(name):
        lo, hi = cols[name]
        return xn @ w_in[:, lo:hi]

    def kv(name):
        return proj(name).reshape(b, s, G, dh)

    q = proj('q').reshape(b, s, G, R, dh) * (HEAD_DIM ** -0.5)
    k_cmp = compress_blocks(kv('k_cmp'), cmp_pos_k, cmp_k_w1, cmp_k_b1, cmp_k_w2)
    v_cmp = compress_blocks(kv('v_cmp'), cmp_pos_v, cmp_v_w1, cmp_v_b1, cmp_v_w2)
    o_nsa = nsa_mixer(q, k_cmp, v_cmp, kv('k_slc'), kv('v_slc'), kv('k_swa'), kv('v_swa'),
                      proj('nsa_gate').reshape(b, s, G, R, 3), rel_bias)
    h_nsa = (o_nsa * jax.nn.silu(proj('z_nsa'))) @ w_out_nsa
    xbc = jax.nn.silu(causal_depthwise_conv(proj('xbc'), conv_w, conv_b))
    xs, bm, cm = jnp.split(xbc, [SSM_WIDTH, SSM_WIDTH + SSM_GROUPS * SSM_STATE], axis=-1)
    dt = jax.nn.softplus(proj('dt').astype(jnp.float32) + dt_bias.astype(jnp.float32)).reshape(b, s, SSM_GROUPS, SSM_HPG)
    a = -jnp.exp(a_log.astype(jnp.float32)).reshape(SSM_GROUPS, SSM_HPG)
    xh = xs.astype(jnp.float32).reshape(b, s, SSM_GROUPS, SSM_HPG, SSM_HEAD_DIM)
    y = ssd_chunked(xh, dt, a, bm.astype(jnp.float32).reshape(b, s, SSM_GROUPS, SSM_STATE),
                    cm.astype(jnp.float32).reshape(b, s, SSM_GROUPS, SSM_STATE))
    y = y + d_skip.astype(jnp.float32).reshape(SSM_GROUPS, SSM_HPG)[..., None] * xh
    y = gated_group_rmsnorm(y.reshape(b, s, SSM_WIDTH), proj('z_ssm'), ssm_norm_w)
    h_ssm = y @ w_out_ssm
    g_nsa, g_ssm = jnp.split(jax.nn.sigmoid(proj('merge_gate')), 2, axis=-1)
    return x + (g_nsa * h_nsa + g_ssm * h_ssm) @ w_out


def setup_inputs(seed: int = 0) -> dict:
    key = jax.random.key(seed)
    ks = jax.random.split(key, 24)
    nrm = lambda k, shape, scale: jax.random.normal(k, shape, jnp.float32) * scale
    dt0 = jnp.exp(jax.random.uniform(ks[13], (DEPTH, SSM_HEADS), jnp.float32) * (math.log(0.1) - math.log(0.001)) + math.log(0.001))
    return {
        'x': nrm(ks[0], (BATCH, SEQ, D_MODEL), 1.0),
        'norm_w': 1.0 + nrm(ks[1], (DEPTH, D_MODEL), 0.02),
        'w_in': nrm(ks[2], (DEPTH, D_MODEL, IN_COLS), D_MODEL ** -0.5),
        'cmp_pos_k': nrm(ks[3], (DEPTH, CMP_BLOCK, HEAD_DIM), 0.1),
        'cmp_pos_v': nrm(ks[4], (DEPTH, CMP_BLOCK, HEAD_DIM), 0.1),
        'cmp_k_w1': nrm(ks[5], (DEPTH, CMP_BLOCK * HEAD_DIM, CMP_HIDDEN), (CMP_BLOCK * HEAD_DIM) ** -0.5),
        'cmp_k_b1': nrm(ks[6], (DEPTH, CMP_HIDDEN), 0.02),
        'cmp_k_w2': nrm(ks[7], (DEPTH, CMP_HIDDEN, HEAD_DIM), CMP_HIDDEN ** -0.5),
        'cmp_v_w1': nrm(ks[8], (DEPTH, CMP_BLOCK * HEAD_DIM, CMP_HIDDEN), (CMP_BLOCK * HEAD_DIM) ** -0.5),
        'cmp_v_b1': nrm(ks[9], (DEPTH, CMP_HIDDEN), 0.02),
        'cmp_v_w2': nrm(ks[10], (DEPTH, CMP_HIDDEN, HEAD_DIM), CMP_HIDDEN ** -0.5),
        'conv_w': nrm(ks[11], (DEPTH, CONV_WIDTH, CONV_DIM), CONV_WIDTH ** -0.5),
        'conv_b': nrm(ks[12], (DEPTH, CONV_DIM), 0.02),
        'dt_bias': dt0 + jnp.log(-jnp.expm1(-dt0)),
        'a_log': jnp.log(jax.random.uniform(ks[14], (DEPTH, SSM_HEADS), jnp.float32, 1.0, 16.0)),
        'd_skip': 1.0 + nrm(ks[15], (DEPTH, SSM_HEADS), 0.02),
        'ssm_norm_w': 1.0 + nrm(ks[16], (DEPTH, SSM_WIDTH), 0.02),
        'w_out_nsa': nrm(ks[17], (DEPTH, NSA_WIDTH, D_MODEL), NSA_WIDTH ** -0.5),
        'w_out_ssm': nrm(ks[18], (DEPTH, SSM_WIDTH, D_MODEL), SSM_WIDTH ** -0.5),
        'w_out': nrm(ks[19], (DEPTH, D_MODEL, D_MODEL), D_MODEL ** -0.5),
        'rel_bias': nrm(ks[20], (REL_BUCKETS, NSA_HEADS), 0.2),
        'final_norm_w': 1.0 + nrm(ks[21], (D_MODEL,), 0.02),
    }


def reference(x, norm_w, w_in, cmp_pos_k, cmp_pos_v, cmp_k_w1, cmp_k_b1, cmp_k_w2, cmp_v_w1, cmp_v_b1, cmp_v_w2,
              conv_w, conv_b, dt_bias, a_log, d_skip, ssm_norm_w, w_out_nsa, w_out_ssm, w_out, rel_bias, final_norm_w):
    for layer in range(DEPTH):
        x = hybrid_layer(x, norm_w[layer], w_in[layer], cmp_pos_k[layer], cmp_pos_v[layer], cmp_k_w1[layer],
                         cmp_k_b1[layer], cmp_k_w2[layer], cmp_v_w1[layer], cmp_v_b1[layer], cmp_v_w2[layer],
                         conv_w[layer], conv_b[layer], dt_bias[layer], a_log[layer], d_skip[layer],
                         ssm_norm_w[layer], w_out_nsa[layer], w_out_ssm[layer], w_out[layer], rel_bias)
    return rms_norm(x, final_norm_w)
```

```python
import math
import numpy as np
import ml_dtypes
import concourse.bass as bass
import concourse.mybir as mybir
from concourse.bass_utils import run_bass_kernel_spmd

F32 = mybir.dt.float32
BF = mybir.dt.bfloat16
AF = mybir.ActivationFunctionType
ALU = mybir.AluOpType
AX = mybir.AxisListType

D = 1024
NH = 16
NG = 4
DH = 64
NCOL = 10832
C_Q, C_KC, C_VC, C_KS, C_VS, C_KW, C_VW, C_GATE, C_ZN, C_ZS, C_XBC, C_DT, C_MG = (
    0, 1024, 1280, 1536, 1792, 2048, 2304, 2560, 2608, 3632, 5680, 8752, 8784)
EPS = 1e-6
NEG = -30000.0
ENGS = ['pe', 'act', 'dve', 'pool', 'sp']


class Buf:
    __slots__ = ('name', 'writers', 'readers')

    def __init__(self, name=''):
        self.name = name
        self.writers = []
        self.readers = []


class Chan:
    def __init__(self, idx):
        self.idx = idx
        self.count = 0
        self.last = None


class Op:
    __slots__ = ('eng', 'fn', 'waits', 'is_dma', 'chan', 'sigval')


class Prog:
    def __init__(self, n_chan):
        self.ops = {e: [] for e in ENGS}
        self.ncomp = {e: 0 for e in ENGS}
        self.known = {e: {} for e in ENGS}
        self.chans = [Chan(i) for i in range(n_chan)]
        self.last_comp = {e: None for e in ENGS}
        self.pending = {e: [] for e in ENGS}
        self.next_chan = 0

    def new_chan(self):
        c = self.chans[self.next_chan]
        self.next_chan += 1
        return c

    def barrier(self):
        deps = [o for o in self.last_comp.values() if o is not None]
        deps += [c.last for c in self.chans if c.last is not None]
        for e in ENGS:
            self.pending[e] = list(deps)

    def _record(self, eng, fn, reads, writes, acc, chan):
        op = Op()
        op.eng = eng
        op.fn = fn
        op.is_dma = chan is not None
        op.chan = chan
        deps = []
        if self.pending[eng]:
            deps += self.pending[eng]
            self.pending[eng] = []
        for r in reads:
            deps += r.writers
            r.readers.append(op)
        for w in writes:
            deps += w.writers
            deps += w.readers
            w.writers = [op]
            w.readers = []
        for w in acc:
            deps += w.readers
            if w.writers:
                deps.append(w.writers[0])
            w.readers = []
            w.writers.append(op)
        if chan is not None and chan.last is not None:
            deps.append(chan.last)
        waits = {}
        kn = self.known[eng]
        for y in deps:
            if y is op:
                continue
            if y.is_dma:
                key = ('c', y.chan.idx)
                val = y.sigval
            else:
                if y.eng == 'pe' and eng == 'pe' and not op.is_dma:
                    continue
                key = ('e', y.eng)
                val = y.sigval
            if val <= 0:
                continue
            if kn.get(key, 0) >= val:
                continue
            if waits.get(key, 0) < val:
                waits[key] = val
        if op.is_dma:
            chan.count += 1
            op.sigval = 16 * chan.count
            chan.last = op
        else:
            self.ncomp[eng] += 1
            op.sigval = self.ncomp[eng]
            self.last_comp[eng] = op
        for k, v in waits.items():
            kn[k] = v
        op.waits = list(waits.items())
        self.ops[eng].append(op)
        return op

    def op(self, eng, fn, reads=(), writes=(), acc=()):
        return self._record(eng, fn, reads, writes, acc, None)

    def dma(self, eng, chan, out, in_, reads=(), writes=(), acc=()):
        return self._record(eng, lambda e, o=out, i=in_: e.dma_start(out=o, in_=i), reads, writes, acc, chan)

    def emit(self, eng, e, esems, csems):
        for op in self.ops[eng]:
            for (kind, k), v in op.waits:
                e.wait_ge(esems[k] if kind == 'e' else csems[k], v)
            ins = op.fn(e)
            if op.is_dma:
                ins.then_inc(csems[op.chan.idx], 16)
            else:
                ins.then_inc(esems[eng], 1)


class Arena:
    def __init__(self, t, ncols):
        self.t = t
        self.n = ncols
        self.off = 0

    def alloc(self, cols, dt):
        w = cols if dt == F32 else (cols + 1) // 2
        a = self.t[:, self.off:self.off + w]
        self.off += w
        assert self.off <= self.n, ("SBUF arena overflow", self.off, self.n)
        return a if dt == F32 else a.bitcast(BF)

    def mark(self):
        return self.off

    def release(self, m):
        self.off = m


def t5_bucket_np(dist):
    import jax
    import jax.numpy as jnp
    with jax.default_device(jax.devices('cpu')[0]):
        d = jnp.maximum(jnp.asarray(dist, dtype=jnp.int32), 0)
        df = jnp.maximum(d, 1).astype(jnp.float32)
        large = 16 + (jnp.log(df / 16) / math.log(1024 / 16) * 16).astype(jnp.int32)
        out = jnp.where(d < 16, d, jnp.minimum(large, 31))
        return np.asarray(out)


def geometry(S):
    g = {}
    g['NQT'] = S // 512
    g['NKT'] = S // 128
    g['NCMP'] = S // 16 - 1
    g['NNT'] = max(1, (g['NCMP'] + 127) // 128)
    g['US'] = 1792
    g['UW'] = 1408
    minc = -31 - 2048 * (g['NNT'] - 1) - 16 * 127
    maxc = 512 * (g['NQT'] - 1) + 511 - 31
    g['MINV'] = min(minc, -511)
    g['MAXV'] = max(maxc, g['US'] - 1 - 384)
    g['LS'] = ((g['MAXV'] - g['MINV'] + 1 + 511) // 512) * 512
    g['LW'] = 1536
    return g


def host_consts(S):
    g = geometry(S)
    c = {}
    c['ident_f'] = np.eye(128, dtype=np.float32)
    c['ident_b'] = np.eye(128, dtype=np.float32).astype(ml_dtypes.bfloat16)
    s = np.arange(128)
    c['U'] = (s[:, None] <= s[None, :]).astype(np.float32)
    c['LST'] = (s[:, None] > s[None, :]).astype(np.float32)
    c['ONES'] = np.ones((128, 128), np.float32)
    E = (np.arange(S)[None, :] // 64 == np.arange(64)[:, None])
    c['Ec'] = E.astype(np.float32).astype(ml_dtypes.bfloat16)
    n = np.arange(g['NNT'] * 128)
    cs = n * 16
    ss = np.arange(64) * 64
    ov = ((cs[:, None] < ss[None, :] + 64) & (cs[:, None] + 32 > ss[None, :]) & (n[:, None] < g['NCMP']))
    c['OV'] = ov.astype(np.float32).reshape(g['NNT'], 128, 64).transpose(1, 0, 2).astype(ml_dtypes.bfloat16).copy()
    t = np.arange(S)
    cur = t // 64
    j = np.arange(64)
    valid = (j[None, :] * 64 <= t[:, None])
    forced = (j[None, :] == 0) | (j[None, :] == cur[:, None]) | (j[None, :] == cur[:, None] - 1)
    add = np.where(valid, np.where(forced, 1.0e4, 0.0), -1.0).astype(np.float32)
    c['ADDc'] = add.reshape(S // 128, 128, 64).transpose(1, 0, 2).copy()
    dS = np.arange(g['LS']) + g['MINV']
    bS = t5_bucket_np(dS)
    ohs = np.zeros((33, g['LS']), np.float32)
    ohs[bS, np.arange(g['LS'])] = 1.0
    ohs[:, dS < 0] = 0.0
    ohs[32, dS < 0] = 1.0
    c['OHS'] = ohs
    dW = np.arange(g['LW']) - 511
    bW = t5_bucket_np(dW)
    ohw = np.zeros((33, g['LW']), np.float32)
    ohw[bW, np.arange(g['LW'])] = 1.0
    bad = (dW < 0) | (dW >= 512)
    ohw[:, bad] = 0.0
    ohw[32, bad] = 1.0
    c['OHW'] = ohw
    assert np.all(t5_bucket_np(np.arange(897, 5000)) == 31)
    return c


CONST_NAMES = ['ident_f', 'ident_b', 'U', 'LST', 'ONES', 'Ec', 'OV', 'ADDc', 'OHS', 'OHW']
BF_CONSTS = {'ident_b', 'Ec', 'OV'}


def build(S, NB, debug=(), stop_after=99):
    geo = geometry(S)
    NQT, NKT, NCMP, NNT = geo['NQT'], geo['NKT'], geo['NCMP'], geo['NNT']
    US, UW, MINV, LS, LW = geo['US'], geo['UW'], geo['MINV'], geo['LS'], geo['LW']
    NTT = S // 128
    nc = bass.Bass("TRN2", target_bir_lowering=False)
    hc = host_consts(S)

    def din(name, shape, dt=F32):
        return nc.dram_tensor(name, list(shape), dt, kind="ExternalInput").ap()

    def dscr(name, shape, dt=BF):
        kind = "ExternalOutput" if name in debug else "Internal"
        return nc.dram_tensor(name, list(shape), dt, kind=kind).ap()

    x_d = din('x', [NB, S, D])
    out_d = nc.dram_tensor('out', [NB, S, D], F32, kind="ExternalOutput").ap()
    w_in_d = din('w_in', [D, NCOL])
    normw_d = din('norm_w', [128, 8])
    posk_d = din('posT_k', [64, 32])
    posv_d = din('posT_v', [64, 32])
    w1k_d = din('w1_k', [64, 32, 256])
    w1v_d = din('w1_v', [64, 32, 256])
    b1k_d = din('b1_k', [128, 2])
    b1v_d = din('b1_v', [128, 2])
    w2k_d = din('w2_k', [128, 2, 64])
    w2v_d = din('w2_v', [128, 2, 64])
    convw_d = din('conv_w', [128, 24, 4])
    convb_d = din('conv_b', [128, 24])
    dtb_d = din('dt_bias', [1, 32])
    alog_d = din('a_log', [1, 32])
    dsk_d = din('d_skip', [1, 32])
    snw_d = din('ssm_norm_w', [128, 16])
    won_d = din('w_out_nsa', [D, D])
    wos_d = din('w_out_ssm', [2 * D, D])
    wo_d = din('w_out', [D, D])
    relb_d = din('rel_bias', [32, 16])
    fnw_d = din('final_norm_w', [1, D])
    cd = {}
    for nme in CONST_NAMES:
        cd[nme] = din('c_' + nme, hc[nme].shape, BF if nme in BF_CONSTS else F32)

    qT_d = dscr('qT', [NB, 1024, S])
    kcT_d = dscr('kcT', [NB, 256, S])
    vcT_d = dscr('vcT', [NB, 256, S])
    ksT_d = dscr('ksT', [NB, 256, S])
    kwT_d = dscr('kwT', [NB, 256, S])
    vs_d = dscr('vs', [NB, S, 256])
    vw_d = dscr('vw', [NB, S, 256])
    gate_d = dscr('gate', [NB, S, 48], F32)
    szT_d = dscr('szT', [NB, 1024, S])
    szs_d = dscr('szs', [NB, S, 2048])
    xbcT_d = dscr('xbcT', [NB, 3072, S])
    dt_d = dscr('dtv', [NB, S, 32], F32)
    gT_d = dscr('gT', [NB, 2048, S])
    ozT_d = dscr('ozT', [NB, 1024, S])
    ynT_d = dscr('ynT', [NB, 2048, S])
    reps_d = dscr('reps', [16, 128, LS])
    repw_d = dscr('repw', [16, 128, LW])

    N_CHAN = 60
    P = Prog(N_CHAN)
    ARENA_COLS = 51 * 1024

    import contextlib
    with contextlib.ExitStack() as es:
        arena_t = es.enter_context(nc.sbuf_tensor("arena", [128, ARENA_COLS], F32))
        banks = [es.enter_context(nc.psum_tensor(f"bank{i}", [128, 512], F32)) for i in range(8)]
        esems = {e: es.enter_context(nc.semaphore("se_" + e)) for e in ENGS}
        csems = [es.enter_context(nc.semaphore(f"sc_{i}")) for i in range(N_CHAN)]
        block = es.enter_context(nc.Block())
        A = Arena(arena_t, ARENA_COLS)
        bankbuf = [Buf(f"bank{i}") for i in range(8)]

        ch_cs = [P.new_chan() for _ in range(4)]
        ch_ci = [0]

        class _RR:
            pass
        ch_c = None
        cbuf = Buf('consts')

        def next_cc():
            ch_ci[0] += 1
            return ch_cs[ch_ci[0] % 4]

        def load_const(ap_d, cols, dt, parts=128, shape3=None):
            t = A.alloc(cols, dt)
            dst = t[0:parts, :]
            src = ap_d
            if shape3 is not None:
                dst = dst.rearrange("p (a b) -> p a b", a=shape3[0])
            P.dma('sp', next_cc(), dst, src, acc=[cbuf])
            return dst

        ident_f = load_const(cd['ident_f'], 128, F32)
        ident_b = load_const(cd['ident_b'], 128, BF)
        U_sb = load_const(cd['U'], 128, F32)
        LST_sb = load_const(cd['LST'], 128, F32)
        ONES_sb = load_const(cd['ONES'], 128, F32)
        OV_sb = load_const(cd['OV'], NNT * 64, BF, shape3=(NNT, 64))
        ADD_sb = load_const(cd['ADDc'], NTT * 64, F32, shape3=(NTT, 64))
        normw_sb = load_const(normw_d, 8, F32)
        convw_sb = load_const(convw_d, 96, F32, shape3=(24, 4))
        convb_sb = load_const(convb_d, 24, F32)
        snw_sb = load_const(snw_d, 16, F32)
        b1k_sb = load_const(b1k_d, 2, F32)
        b1v_sb = load_const(b1v_d, 2, F32)
        dtb_bc = load_const(dtb_d[0:1, :].partition_broadcast(128), 32, F32)
        alog_bc = load_const(alog_d[0:1, :].partition_broadcast(128), 32, F32)
        dsk_bc = load_const(dsk_d[0:1, :].partition_broadcast(128), 32, F32)
        b31_bc = load_const(relb_d[31:32, :].partition_broadcast(128), 16, F32)
        fnw_bc = load_const(fnw_d[0:1, :].partition_broadcast(128), D, F32)
        relb_sb = A.alloc(16, F32)
        P.dma('sp', next_cc(), relb_sb[0:32, :], relb_d, acc=[cbuf])
        zero_b = A.alloc(512, BF)
        A_bc = A.alloc(32, F32)
        tinyc = A.alloc(2, F32)
        P.op('dve', lambda e: e.memset(relb_sb[32:33, :], NEG), acc=[cbuf])
        P.op('dve', lambda e: e.memset(zero_b, 0.0), acc=[cbuf])
        P.op('dve', lambda e: e.memset(tinyc, 0.0), acc=[cbuf])
        P.op('act', lambda e: e.activation(out=A_bc, in_=alog_bc, func=AF.Exp), reads=[cbuf], acc=[cbuf])
        P.op('dve', lambda e: e.tensor_scalar(A_bc, A_bc, -1.0, None, ALU.mult), reads=[cbuf], acc=[cbuf])
        if 'touch_top' in debug:
            P.op('dve', lambda e: e.memset(arena_t[:, ARENA_COLS - 6144:ARENA_COLS], 0.0), acc=[cbuf])
        persist_mark = A.mark()

        def setup_rep():
            m0 = A.mark()
            relrep = A.alloc(16 * 128, F32).rearrange("p (h c) -> p h c", h=16)
            tb = Buf('relrep')
            P.op('dve', lambda e: e.tensor_copy(out=relrep[0:33], in_=relb_sb[0:33, :].unsqueeze(2).to_broadcast([33, 16, 128])),
                 reads=[cbuf], writes=[tb])
            ch_oh = P.new_chan()
            ch_st = [P.new_chan(), P.new_chan()]
            for (oh_d, L, rep_d, nm) in ((cd['OHS'], LS, reps_d, 's'), (cd['OHW'], LW, repw_d, 'w')):
                oh = A.alloc(L, F32)
                ohb = Buf('oh')
                P.dma('sp', ch_oh, oh[0:33, :], oh_d, writes=[ohb])
                stg = [A.alloc(L, BF) for _ in range(2)]
                stb = [Buf('repst0'), Buf('repst1')]
                for h in range(16):
                    sl = h % 2
                    for cidx in range(L // 512):
                        bk = (h * (L // 512) + cidx) % 4
                        P.op('pe', lambda e, bk=bk, h=h, cidx=cidx, oh=oh: e.matmul(
                            banks[bk][:, :], lhsT=relrep[0:33, h, :], rhs=oh[0:33, cidx * 512:(cidx + 1) * 512],
                            start=True, stop=True), reads=[tb, ohb], writes=[bankbuf[bk]])
                        P.op('act', lambda e, bk=bk, sl=sl, cidx=cidx, stg=stg: e.activation(
                            out=stg[sl][:, cidx * 512:(cidx + 1) * 512], in_=banks[bk][:, :], func=AF.Exp),
                            reads=[bankbuf[bk]], writes=[stb[sl]] if cidx == 0 else [], acc=[] if cidx == 0 else [stb[sl]])
                    P.dma('sp', ch_st[sl], rep_d[h], stg[sl][:, 0:L], reads=[stb[sl]], acc=[repbuf])
            A.release(m0)

        repbuf = Buf('rep')
        setup_rep()
        P.barrier()

        wobuf = Buf('wout')
        wo_chs = [P.new_chan(), P.new_chan()]

        def load_out_weights(Wn, Ws, Wo):
            st = [A.alloc(2048, F32) for _ in range(2)]
            stb = [Buf('wst0'), Buf('wst1')]
            chs = wo_chs
            i = 0
            first = [True]
            for (wd, wsb, nk, scaled) in ((won_d, Wn, 8, False), (wos_d, Ws, 16, True), (wo_d, Wo, 8, False)):
                wv = wd.rearrange("(k p) c -> p k c", p=128)
                for k0 in range(0, nk, 2):
                    sl = i % 2
                    i += 1
                    s3 = st[sl].rearrange("p (k c) -> p k c", k=2)
                    P.dma('sp', chs[sl], s3, wv[:, k0:k0 + 2, :], writes=[stb[sl]])
                    wr = dict(writes=[wobuf]) if first[0] else dict(acc=[wobuf])
                    first[0] = False
                    if scaled:
                        P.op('pool', lambda e, s3=s3, k0=k0, wsb=wsb: e.tensor_tensor(
                            out=wsb[:, k0:k0 + 2, :], in0=s3, in1=snw_sb[:, k0:k0 + 2].unsqueeze(2).to_broadcast([128, 2, 1024]),
                            op=ALU.mult), reads=[stb[sl], cbuf], **wr)
                    else:
                        P.op('pool', lambda e, s3=s3, k0=k0, wsb=wsb: e.tensor_copy(out=wsb[:, k0:k0 + 2, :], in_=s3),
                             reads=[stb[sl]], **wr)


        CH = {}

        def chan(name):
            if name not in CH:
                CH[name] = P.new_chan()
            return CH[name]

        def cp(eng, dst, src, reads=(), writes=(), acc=()):
            if eng == 'act':
                P.op('act', lambda e: e.activation(out=dst, in_=src, func=AF.Copy), reads=reads, writes=writes, acc=acc)
            else:
                P.op(eng, lambda e: e.tensor_copy(out=dst, in_=src), reads=reads, writes=writes, acc=acc)

        def do_nsa(b):
            m0 = A.mark()
            NP = NNT * 128
            KcT = A.alloc(4 * NP, BF).rearrange("p (g n) -> p g n", g=4)
            Vc = A.alloc(4 * NNT * 65, BF).rearrange("p (g t c) -> p g t c", g=4, t=NNT)
            kcb = Buf('KcT')
            vcb = Buf('Vc')
            P.op('dve', lambda e: e.memset(KcT, 0.0), writes=[kcb])
            P.op('pool', lambda e: e.memset(Vc, 1.0), writes=[vcb])
            mc = A.mark()
            KC = A.alloc(4 * S, BF).rearrange("p (g s) -> p g s", g=4)
            KCb = Buf('KC')
            w1f = A.alloc(32 * 256, F32)
            w1fb = Buf('w1f')
            w1 = A.alloc(32 * 256, BF).rearrange("p (l c) -> p l c", l=32)
            w1b = Buf('w1')
            posf = A.alloc(34, F32)
            posb_ = A.alloc(34, BF)
            posB = Buf('pos')
            w2f = A.alloc(128, F32)
            w2 = A.alloc(128, BF).rearrange("p (c d) -> p c d", c=2)
            w2B = Buf('w2')
            bias_sb = A.alloc(2, F32)
            biasB = Buf('bias')
            hT = A.alloc(2 * 4 * NP, BF).rearrange("p (c g n) -> p c g n", c=2, g=4)
            hTb = Buf('hT')
            for kind in ('k', 'v'):
                src_d, w1_d, pos_d, b1_sb, w2_d = ((kcT_d, w1k_d, posk_d, b1k_sb, w2k_d) if kind == 'k'
                                                   else (vcT_d, w1v_d, posv_d, b1v_sb, w2v_d))
                P.dma('sp', chan('kc'), KC[0:64], src_d[b].rearrange("(g d) s -> d g s", d=64), reads=[scr[b]], writes=[KCb])
                P.dma('sp', chan('w1'), w1f[0:64, :], w1_d.rearrange("d l c -> d (l c)"), writes=[w1fb])
                P.op('pool', lambda e: e.tensor_copy(out=w1[0:64].rearrange("p l c -> p (l c)"), in_=w1f[0:64, :]),
                     reads=[w1fb], writes=[w1b])
                P.op('dve', lambda e: e.memset(posf[0:64, :], 0.0), writes=[posB])
                P.dma('sp', chan('w1'), posf[0:64, 0:32], pos_d, reads=[posB], acc=[posB])
                P.op('dve', lambda e: e.tensor_copy(out=posb_[0:64, :], in_=posf[0:64, :]), reads=[posB], acc=[posB])
                P.dma('sp', chan('w1'), w2f, w2_d.rearrange("p c d -> p (c d)"), writes=[w2B])
                P.op('dve', lambda e: e.tensor_copy(out=w2.rearrange("p c d -> p (c d)"), in_=w2f), reads=[w2B], acc=[w2B])
                P.op('pool', lambda e: e.memset(hT, 0.0), writes=[hTb])
                for c in range(2):
                    for l in range(32):
                        P.op('pe', lambda e, c=c, l=l: e.matmul(banks[7][:, 0:2], lhsT=w1[0:64, l, c * 128:(c + 1) * 128],
                                                               rhs=posb_[0:64, l:l + 2], start=(l == 0), stop=(l == 31)),
                             reads=[w1b, posB], writes=[bankbuf[7]] if l == 0 else [], acc=[] if l == 0 else [bankbuf[7]])
                    P.op('dve', lambda e, c=c, b1_sb=b1_sb: e.tensor_tensor(out=bias_sb[:, c:c + 1], in0=banks[7][:, 0:1],
                                                                           in1=b1_sb[:, c:c + 1], op=ALU.add),
                         reads=[bankbuf[7], cbuf], writes=[biasB] if c == 0 else [], acc=[] if c == 0 else [biasB])
                cnt = 0
                for c in range(2):
                    for gp in range(2):
                        bk = cnt % 2
                        cnt += 1
                        ov = banks[bk][:, 0:2 * NCMP].rearrange("p (g n) -> p g n", g=2)
                        for l in range(32):
                            P.op('pe', lambda e, c=c, gp=gp, l=l, ov=ov: e.matmul(
                                ov, lhsT=w1[0:64, l, c * 128:(c + 1) * 128],
                                rhs=KC[0:64, 2 * gp:2 * gp + 2, l:l + 16 * (NCMP - 1) + 1:16], start=(l == 0), stop=(l == 31)),
                                reads=[w1b, KCb], writes=[bankbuf[bk]] if l == 0 else [], acc=[] if l == 0 else [bankbuf[bk]])
                        P.op('act', lambda e, c=c, gp=gp, ov=ov: e.activation(
                            out=hT[:, c, 2 * gp:2 * gp + 2, 0:NCMP], in_=ov, func=AF.Silu, bias=bias_sb[:, c:c + 1]),
                            reads=[bankbuf[bk], biasB, hTb], acc=[hTb])
                if kind == 'k':
                    for gp in range(2):
                        ov = banks[2 + gp][0:64, 0:2 * NCMP].rearrange("p (g n) -> p g n", g=2)
                        for c in range(2):
                            P.op('pe', lambda e, c=c, gp=gp, ov=ov: e.matmul(
                                ov, lhsT=w2[:, c, :], rhs=hT[:, c, 2 * gp:2 * gp + 2, 0:NCMP], start=(c == 0), stop=(c == 1)),
                                reads=[w2B, hTb], writes=[bankbuf[2 + gp]] if c == 0 else [], acc=[] if c == 0 else [bankbuf[2 + gp]])
                        cp('dve', KcT[0:64, 2 * gp:2 * gp + 2, 0:NCMP], ov, reads=[bankbuf[2 + gp], kcb], acc=[kcb])
                else:
                    cnt = 0
                    for g in range(4):
                        for NT in range(NNT):
                            bk = 2 + cnt % 2
                            cnt += 1
                            for c in range(2):
                                P.op('pe', lambda e, c=c, g=g, NT=NT, bk=bk: e.matmul(
                                    banks[bk][:, 0:64], lhsT=hT[:, c, g, NT * 128:(NT + 1) * 128], rhs=w2[:, c, :],
                                    start=(c == 0), stop=(c == 1)),
                                    reads=[w2B, hTb], writes=[bankbuf[bk]] if c == 0 else [], acc=[] if c == 0 else [bankbuf[bk]])
                            cp('dve', Vc[:, g, NT, 0:64], banks[bk][:, 0:64], reads=[bankbuf[bk], vcb], acc=[vcb])
            if 'dbg_kc' in debug and b == 0:
                dk = nc.dram_tensor('dbg_kc', [64, 4, NP], BF, kind="ExternalOutput").ap()
                dv = nc.dram_tensor('dbg_vc', [128, 4, NNT, 65], BF, kind="ExternalOutput").ap()
                P.dma('sp', next_cc(), dk, KcT[0:64], reads=[kcb])
                P.dma('sp', next_cc(), dv, Vc, reads=[vcb])
            P.barrier()
            A.release(mc)
            if stop_after <= 3:
                A.release(m0)
                return

            Kaug = A.alloc(S, BF)
            Kw = A.alloc(S, BF)
            Vs = A.alloc(NKT * 65, BF).rearrange("p (t c) -> p t c", t=NKT)
            Vw = A.alloc(NKT * 65, BF).rearrange("p (t c) -> p t c", t=NKT)
            EBs = A.alloc(4 * US, BF).rearrange("p (h u) -> p h u", h=4)
            EBw = A.alloc(4 * UW, BF).rearrange("p (h u) -> p h u", h=4)
            grpB = Buf('grp')
            Qaug = [A.alloc(4 * 512, BF).rearrange("p (h t) -> p h t", h=4) for _ in range(2)]
            Qb = [Buf('Qa0'), Buf('Qa1')]
            Gt = [A.alloc(4 * 48, F32).rearrange("p (s c) -> p s c", s=4) for _ in range(2)]
            SZ = [A.alloc(2 * 512, BF).rearrange("p (f t) -> p f t", f=2) for _ in range(2)]
            qinB = [Buf('qin0'), Buf('qin1')]
            EBc = [A.alloc(4 * 512, BF).rearrange("p (h t) -> p h t", h=4) for _ in range(2)]
            EBcB = [Buf('ebc0'), Buf('ebc1')]
            Pt = [A.alloc(512, BF) for _ in range(4)]
            Pb = [Buf(f'P{i}') for i in range(4)]
            o_acc = A.alloc(1024, F32)
            o4 = o_acc.rearrange("p (s h d) -> p s h d", s=4, h=4)
            oaB = Buf('oacc')
            imp = A.alloc(256, F32).rearrange("p (s j) -> p s j", s=4)
            impB = Buf('imp')
            tmpA = [A.alloc(256, F32).rearrange("p (s j) -> p s j", s=4) for _ in range(2)]
            tmpB = [Buf('tmpA0'), Buf('tmpA1')]
            m8 = A.alloc(32, F32).rearrange("p (s k) -> p s k", s=4)
            thr = A.alloc(4, F32)
            selB = Buf('sel')
            selpad = A.alloc(512, BF).rearrange("p (s c) -> p s c", s=4)
            rs = [A.alloc(4, F32) for _ in range(2)]
            cf = [A.alloc(4, F32) for _ in range(2)]
            rsB = [Buf('rs0'), Buf('rs1')]
            ozst = [A.alloc(512, BF) for _ in range(2)]
            ozB = [Buf('oz0'), Buf('oz1')]
            P.op('dve', lambda e: e.memset(selpad, 0.0), writes=[selB])
            P.op('pool', lambda e: e.memset(Vs, 1.0), writes=[grpB])
            P.op('pool', lambda e: e.memset(Vw, 1.0), acc=[grpB])
            Tb = banks[6][:, 0:256].bitcast(BF)
            T32 = banks[6]
            qcnt = 0
            ocnt = 0
            pcnt = 0
            ecnt = 0
            mulc = 0
            ozc = 0
            def do_qt(g, QT):
                nonlocal qcnt, ocnt, pcnt, ecnt, mulc, ozc
                qs = qcnt % 2
                qcnt += 1
                Qa = Qaug[qs]
                G = Gt[qs]
                P.dma('sp', chan(f'q{qs}'), Qa[0:64], qT_d[b, g * 256:(g + 1) * 256, QT * 512:(QT + 1) * 512]
                      .rearrange("(h d) t -> d h t", d=64), reads=[scr[b]], writes=[Qb[qs]])
                P.dma('sp', chan(f'q{qs}'), G, gate_d[b, QT * 512:(QT + 1) * 512, :].rearrange("(s p) c -> p s c", p=128),
                      reads=[scr[b]], writes=[qinB[qs]])
                P.dma('sp', chan(f'q{qs}'), SZ[qs], szT_d[b, g * 256:(g + 1) * 256, QT * 512:(QT + 1) * 512]
                      .rearrange("(f p) t -> p f t", p=128), reads=[scr[b]], acc=[qinB[qs]])
                items = []
                nts = [NT for NT in range(NNT) if 2048 * NT + 31 <= 512 * QT + 511]
                for h in range(4):
                    for i, NT in enumerate(nts):
                        items.append(dict(br=0, h=h, kt=NT, first=(i == 0), last=(i == len(nts) - 1)))
                def slc_sw(h):
                    kts = list(range(0, 4 * QT + 4))
                    for i, KT in enumerate(kts):
                        items.append(dict(br=1, h=h, kt=KT, first=(i == 0), last=(i == len(kts) - 1)))
                    kts = list(range(max(0, 4 * QT - 4), 4 * QT + 4))
                    for i, KT in enumerate(kts):
                        items.append(dict(br=2, h=h, kt=KT, first=(i == 0), last=(i == len(kts) - 1)))
                for h in range(4):
                    slc_sw(h)
                ebc_slot = {}
                for NT in nts:
                    es_ = ecnt % 2
                    ecnt += 1
                    c0 = 512 * QT - 2048 * NT - 31
                    P.dma('sp', chan(f'ebc{es_}'), EBc[es_], bass.AP(reps_d.tensor, g * 4 * 128 * LS + (c0 - MINV),
                                                                      [[LS - 16, 128], [128 * LS, 4], [1, 512]]),
                          reads=[repbuf], writes=[EBcB[es_]])
                    ebc_slot[NT] = es_

                def front(it):
                    nonlocal pcnt, mulc, ocnt
                    si = pcnt % 2
                    pi = pcnt % 4
                    pcnt += 1
                    it['pi'] = pi
                    h, KT, br = it['h'], it['kt'], it['br']
                    hg = g * 4 + h
                    if it['first']:
                        it['ob'] = ocnt % 2
                        ocnt += 1
                    else:
                        it['ob'] = it['prev']['ob']
                    sb = banks[si]
                    if br == 0:
                        P.op('pe', lambda e: e.matmul(sb[:, :], lhsT=KcT[0:64, g, KT * 128:(KT + 1) * 128],
                                                     rhs=Qa[0:64, h, :], start=True, stop=True),
                             reads=[kcb, Qb[qs]], writes=[bankbuf[si]])
                    elif br == 1:
                        P.op('pe', lambda e: e.matmul(sb[:, :], lhsT=Kaug[:, KT * 128:(KT + 1) * 128],
                                                     rhs=Qa[:, h, :], start=True, stop=True),
                             reads=[grpB, Qb[qs]], writes=[bankbuf[si]])
                    else:
                        P.op('pe', lambda e: e.matmul(sb[:, :], lhsT=Kw[0:64, KT * 128:(KT + 1) * 128],
                                                     rhs=Qa[0:64, h, :], start=True, stop=True),
                             reads=[grpB, Qb[qs]], writes=[bankbuf[si]])
                    off = 512 * QT - 128 * KT
                    far = (br == 1 and off >= 1024)
                    if far:
                        P.op('act', lambda e: e.activation(out=Pt[pi], in_=sb[:, :], func=AF.Exp, bias=b31_bc[:, hg:hg + 1]),
                             reads=[bankbuf[si], cbuf], writes=[Pb[pi]])
                    else:
                        P.op('act', lambda e: e.activation(out=Pt[pi], in_=sb[:, :], func=AF.Exp),
                             reads=[bankbuf[si]], writes=[Pb[pi]])
                        if br == 0:
                            ebt = EBc[ebc_slot[KT]][:, h, :]
                            rd = [EBcB[ebc_slot[KT]]]
                        elif br == 1:
                            ebt = EBs[:, h, off + 384:off + 384 + 512]
                            rd = [grpB]
                        else:
                            ebt = EBw[:, h, off + 384:off + 384 + 512]
                            rd = [grpB]
                        eng = 'dve' if mulc % 2 == 0 else 'pool'
                        mulc += 1
                        P.op(eng, lambda e: e.tensor_tensor(out=Pt[pi], in0=Pt[pi], in1=ebt, op=ALU.mult),
                             reads=rd + [Pb[pi]], acc=[Pb[pi]])

                def back(it):
                    h, KT, br, pi, ob = it['h'], it['kt'], it['br'], it['pi'], it['ob']
                    hg = g * 4 + h
                    Ob = banks[2 + ob]
                    Ib = banks[4 + ob]
                    if it['first']:
                        P.op('pe', lambda e: e.matmul(Ob[:, 0:260], lhsT=zero_b[0:1, 0:128], rhs=zero_b[0:1, 0:260],
                                                     start=True, stop=False, skip_group_check=True),
                             reads=[cbuf], writes=[bankbuf[2 + ob]])
                        if br == 0:
                            P.op('pe', lambda e: e.matmul(Ib[:, 0:256], lhsT=zero_b[0:1, 0:128], rhs=zero_b[0:1, 0:256],
                                                         start=True, stop=False, skip_group_check=True),
                                 reads=[cbuf], writes=[bankbuf[4 + ob]])
                    for sub in range(4):
                        if br == 1 and KT > 4 * QT + sub:
                            continue
                        if br == 2 and (KT > 4 * QT + sub or KT < 4 * QT + sub - 4):
                            continue
                        if br == 0:
                            rhs = Vc[:, g, KT, :]
                            rd = [vcb]
                        elif br == 1:
                            rhs = Vs[:, KT, :]
                            rd = [grpB]
                        else:
                            rhs = Vw[:, KT, :]
                            rd = [grpB]
                        P.op('pe', lambda e, sub=sub, rhs=rhs: e.matmul(
                            Ob[:, sub * 65:(sub + 1) * 65], lhsT=Pt[pi][:, sub * 128:(sub + 1) * 128], rhs=rhs,
                            start=False, stop=False, skip_group_check=True),
                            reads=[Pb[pi]] + rd, acc=[bankbuf[2 + ob]])
                        if br == 0:
                            P.op('pe', lambda e, sub=sub: e.matmul(
                                Ib[:, sub * 64:(sub + 1) * 64], lhsT=Pt[pi][:, sub * 128:(sub + 1) * 128], rhs=OV_sb[:, KT, :],
                                start=False, stop=False, skip_group_check=True),
                                reads=[Pb[pi], cbuf], acc=[bankbuf[4 + ob]])
                    if not it['last']:
                        return
                    r = ob
                    O3 = Ob[:, 0:260].rearrange("p (s c) -> p s c", s=4)
                    P.op('dve', lambda e: e.tensor_scalar(rs[r], O3[:, :, 64], 1e-30, None, ALU.max),
                         reads=[bankbuf[2 + ob]], writes=[rsB[r]])
                    P.op('dve', lambda e: e.reciprocal(rs[r], rs[r]), reads=[rsB[r]], acc=[rsB[r]])
                    gi = g * 12 + h * 3 + br
                    P.op('dve', lambda e: e.tensor_tensor(out=cf[r], in0=rs[r], in1=G[:, :, gi], op=ALU.mult),
                         reads=[qinB[qs], rsB[r]], acc=[rsB[r]])
                    cfb = cf[r].unsqueeze(2).to_broadcast([128, 4, 64])
                    if br == 0:
                        P.op('dve', lambda e: e.tensor_tensor(out=o4[:, :, h, :], in0=O3[:, :, 0:64], in1=cfb, op=ALU.mult),
                             reads=[bankbuf[2 + ob], rsB[r]], writes=[oaB] if h == 0 else [], acc=[] if h == 0 else [oaB])
                        I3 = Ib[:, 0:256].rearrange("p (s j) -> p s j", s=4)
                        rsb = rs[r].unsqueeze(2).to_broadcast([128, 4, 64])
                        if h == 0:
                            P.op('dve', lambda e: e.tensor_tensor(out=imp, in0=I3, in1=rsb, op=ALU.mult),
                                 reads=[bankbuf[4 + ob], rsB[r]], writes=[impB])
                        else:
                            P.op('dve', lambda e: e.tensor_tensor(out=tmpA[r], in0=I3, in1=rsb, op=ALU.mult),
                                 reads=[bankbuf[4 + ob], rsB[r]], writes=[tmpB[r]])
                            P.op('pool', lambda e: e.tensor_tensor(out=imp, in0=imp, in1=tmpA[r], op=ALU.add),
                                 reads=[tmpB[r], impB], acc=[impB])
                    else:
                        P.op('dve', lambda e: e.tensor_tensor(out=tmpA[r], in0=O3[:, :, 0:64], in1=cfb, op=ALU.mult),
                             reads=[bankbuf[2 + ob], rsB[r]], writes=[tmpB[r]])
                        P.op('pool', lambda e: e.tensor_tensor(out=o4[:, :, h, :], in0=o4[:, :, h, :], in1=tmpA[r], op=ALU.add),
                             reads=[tmpB[r], oaB], acc=[oaB])

                def selection():
                    P.op('dve', lambda e: e.tensor_tensor(out=imp, in0=imp, in1=ADD_sb[:, 4 * QT:4 * QT + 4, :], op=ALU.add),
                         reads=[cbuf, impB], acc=[impB])
                    for sub in range(4):
                        P.op('dve', lambda e, sub=sub: e.max(out=m8[:, sub, :], in_=imp[:, sub, :]), reads=[impB], acc=[selB])
                    P.op('dve', lambda e: e.tensor_scalar(thr, m8[:, :, 7], 0.0, None, ALU.max), reads=[selB], acc=[selB])
                    for sub in range(4):
                        P.op('dve', lambda e, sub=sub: e.tensor_scalar(selpad[:, sub, 64:128], imp[:, sub, :],
                                                                       thr[:, sub:sub + 1], NEG, ALU.is_lt, ALU.mult),
                             reads=[impB, selB], acc=[selB])
                    for sub in range(4):
                        P.op('pe', lambda e, sub=sub: e.transpose(out=Tb[:, sub * 128:(sub + 1) * 128], in_=selpad[:, sub, :],
                                                                 identity=ident_b),
                             reads=[selB, cbuf], writes=[bankbuf[6]] if sub == 0 else [], acc=[] if sub == 0 else [bankbuf[6]])
                    P.op('act', lambda e: e.activation(out=Qa[64:128, :, :], in_=Tb[64:128, 0:512].unsqueeze(1).to_broadcast([64, 4, 512]),
                                                      func=AF.Copy),
                         reads=[bankbuf[6], Qb[qs]], acc=[Qb[qs]])

                prev = None
                last_of_group = {}
                for idx, it in enumerate(items):
                    it['prev'] = items[idx - 1] if idx > 0 else None
                n_cmp_items = 4 * len(nts)
                for idx in range(len(items) + 1):
                    if idx == n_cmp_items:
                        if prev is not None:
                            back(prev)
                            prev = None
                        selection()
                    if idx < len(items):
                        front(items[idx])
                    if prev is not None:
                        back(prev)
                    prev = items[idx] if idx < len(items) else None
                for fc in range(2):
                    for sub in range(4):
                        P.op('pe', lambda e, sub=sub, fc=fc: e.transpose(
                            out=T32[:, sub * 128:(sub + 1) * 128], in_=o_acc[:, sub * 256 + fc * 128:sub * 256 + (fc + 1) * 128],
                            identity=ident_f),
                            reads=[oaB, cbuf], writes=[bankbuf[6]] if sub == 0 else [], acc=[] if sub == 0 else [bankbuf[6]])
                    zs = ozc % 2
                    ozc += 1
                    P.op('dve', lambda e, fc=fc, zs=zs: e.tensor_tensor(out=ozst[zs], in0=T32[:, :], in1=SZ[qs][:, fc, :], op=ALU.mult),
                         reads=[bankbuf[6], qinB[qs]], writes=[ozB[zs]])
                    P.dma('pool', chan(f'oz{zs}'), ozT_d[b, (g * 2 + fc) * 128:(g * 2 + fc + 1) * 128, QT * 512:(QT + 1) * 512],
                          ozst[zs], reads=[ozB[zs]], acc=[scr2[b]])

            for g in range(4):
                P.dma('sp', chan('g0'), Kaug[0:64, :], ksT_d[b, g * 64:(g + 1) * 64, :], reads=[scr[b]], writes=[grpB])
                P.dma('sp', chan('g0'), Kaug[64:128, :], cd['Ec'], acc=[grpB])
                P.dma('sp', chan('g0'), Kw[0:64, :], kwT_d[b, g * 64:(g + 1) * 64, :], reads=[scr[b]], acc=[grpB])
                P.dma('sp', chan('g0'), Vs[:, :, 0:64], vs_d[b, :, g * 64:(g + 1) * 64].rearrange("(t p) c -> p t c", p=128),
                      reads=[scr[b]], acc=[grpB])
                P.dma('sp', chan('g0'), Vw[:, :, 0:64], vw_d[b, :, g * 64:(g + 1) * 64].rearrange("(t p) c -> p t c", p=128),
                      reads=[scr[b]], acc=[grpB])
                P.dma('sp', chan('g0'), EBs, bass.AP(reps_d.tensor, g * 4 * 128 * LS + (-384 - MINV),
                                                      [[LS - 1, 128], [128 * LS, 4], [1, US]]), reads=[repbuf], acc=[grpB])
                P.dma('sp', chan('g0'), EBw, bass.AP(repw_d.tensor, g * 4 * 128 * LW + 127,
                                                      [[LW - 1, 128], [128 * LW, 4], [1, UW]]), reads=[repbuf], acc=[grpB])
                for QT in range(NQT):
                    do_qt(g, QT)
            P.barrier()
            A.release(m0)


        def do_ssd(b):
            m0 = A.mark()
            NBLK = S // 512
            XP = [A.alloc(24 * 515, BF).rearrange("p (c t) -> p c t", c=24)] * 2
            _xpb = Buf('XP0')
            XPb = [_xpb, _xpb]
            XC = A.alloc(24 * 512, BF).rearrange("p (c t) -> p c t", c=24)
            XCb = Buf('XC')
            acc32 = [A.alloc(512, F32) for _ in range(2)]
            accB = [Buf('acc0'), Buf('acc1')]
            XS = A.alloc(4 * 2048, BF).rearrange("p (s f) -> p s f", s=4)
            XSb = Buf('XS')
            BT = A.alloc(4 * 512, BF).rearrange("p (s f) -> p s f", s=4)
            BTb = Buf('BT')
            DT = [A.alloc(4 * 32, F32).rearrange("p (s h) -> p s h", s=4)] * 2
            SZS = [A.alloc(4 * 2048, BF).rearrange("p (s f) -> p s f", s=4)] * 2
            _inb = Buf('ssdin0')
            inB = [_inb, _inb]
            st32 = A.alloc(2048, F32)
            stbf = A.alloc(2048, BF)
            stB = Buf('state')
            stbB = Buf('statebf')
            dtA = A.alloc(32, F32)
            cstot = A.alloc(64, F32)
            ecs = A.alloc(64, F32)
            toend = A.alloc(32, F32)
            smB = Buf('ssm_small')
            xD = A.alloc(2048, BF)
            xw = A.alloc(2048, BF)
            xdB = Buf('xD')
            CBm = A.alloc(128, F32)
            CBb = Buf('CBm')
            LH = [A.alloc(128, F32) for _ in range(4)]
            LHb = [Buf(f'LH{i}') for i in range(4)]
            ED = [A.alloc(512, F32) for _ in range(2)]
            EDb = [Buf('ED0'), Buf('ED1')]
            WT = [A.alloc(128, BF) for _ in range(8)]
            WTb = [Buf(f'WT{i}') for i in range(8)]
            Ysb = A.alloc(512, F32)
            YsB = Buf('Ysb')
            ytmp = A.alloc(512, F32)
            ytB = Buf('ytmp')
            yfull = A.alloc(2048, F32)
            yfB = Buf('yfull')
            stmp = A.alloc(512, F32)
            stmpB = Buf('stmp')
            ss4 = A.alloc(16, F32)
            ssB = Buf('ss4')
            sqj = A.alloc(512, BF)
            sqjB = Buf('sqj')
            hn = A.alloc(2048, BF)
            hnB = Buf('hn')
            ynst = [A.alloc(16 * 512, BF).rearrange("p (c t) -> p c t", c=16)] * 2
            _ynb = Buf('yn0')
            ynB = [_ynb, _ynb]
            P.op('dve', lambda e: e.memset(st32, 0.0), writes=[stB])
            P.op('dve', lambda e: e.memset(stbf, 0.0), writes=[stbB])
            P.op('dve', lambda e: e.memset(XP[0][:, :, 0:3], 0.0), writes=[XPb[0]])
            Tb8 = banks[7][:, :].bitcast(BF)

            def do_chunk(blk, sub, sl, ysl):
                dts = DT[sl][:, sub, :]
                P.op('dve', lambda e: e.tensor_tensor(out=dtA, in0=dts, in1=A_bc, op=ALU.mult), reads=[inB[sl], cbuf], writes=[smB])
                P.op('pe', lambda e: e.matmul(banks[0][:, 0:32], lhsT=U_sb, rhs=dtA, start=True, stop=True),
                     reads=[smB, cbuf], writes=[bankbuf[0]])
                P.op('pe', lambda e: e.matmul(banks[0][:, 32:64], lhsT=ONES_sb, rhs=dtA, start=True, stop=True),
                     reads=[smB, cbuf], acc=[bankbuf[0]])
                P.op('dve', lambda e: e.tensor_copy(out=cstot, in_=banks[0][:, 0:64]), reads=[bankbuf[0], smB], acc=[smB])
                P.op('act', lambda e: e.activation(out=ecs, in_=cstot, func=AF.Exp), reads=[smB], acc=[smB])
                P.op('dve', lambda e: e.tensor_tensor(out=toend, in0=cstot[:, 32:64], in1=cstot[:, 0:32], op=ALU.subtract),
                     reads=[smB], acc=[smB])
                P.op('act', lambda e: e.activation(out=toend, in_=toend, func=AF.Exp), reads=[smB], acc=[smB])
                P.op('dve', lambda e: e.tensor_tensor(out=toend, in0=toend, in1=dts, op=ALU.mult), reads=[smB, inB[sl]], acc=[smB])
                xs3 = XS[:, sub, :].rearrange("p (h d) -> p h d", h=32)
                P.op('pool', lambda e: e.tensor_tensor(out=xD.rearrange("p (h d) -> p h d", h=32), in0=xs3,
                                                       in1=dsk_bc[:, 0:32].unsqueeze(2).to_broadcast([128, 32, 64]), op=ALU.mult),
                     reads=[XSb, cbuf], writes=[xdB])
                P.op('pool', lambda e: e.tensor_tensor(out=xw.rearrange("p (h d) -> p h d", h=32), in0=xs3,
                                                       in1=toend[:, 0:32].unsqueeze(2).to_broadcast([128, 32, 64]), op=ALU.mult),
                     reads=[XSb, smB, xdB], acc=[xdB])
                tsl = slice(sub * 128, (sub + 1) * 128)
                for g in range(4):
                    do_group(blk, sub, sl, g, tsl, dts)
                P.op('dve', lambda e: e.tensor_tensor(out=yfull, in0=yfull, in1=SZS[sl][:, sub, :], op=ALU.mult),
                     reads=[yfB, inB[sl]], acc=[yfB])
                for g in range(4):
                    P.op('act', lambda e, g=g: e.activation(out=sqj, in_=yfull[:, g * 512:(g + 1) * 512], func=AF.Square,
                                                           accum_out=ss4[:, g:g + 1]),
                         reads=[yfB], writes=[sqjB, ssB] if g == 0 else [sqjB], acc=[ssB] if g else [])
                P.op('dve', lambda e: e.tensor_scalar(ss4[:, 4:8], ss4[:, 0:4], 1.0 / 512, EPS, ALU.mult, ALU.add), reads=[ssB], acc=[ssB])
                P.op('act', lambda e: e.activation(out=ss4[:, 8:12], in_=ss4[:, 4:8], func=AF.Sqrt), reads=[ssB], acc=[ssB])
                P.op('dve', lambda e: e.reciprocal(ss4[:, 12:16], ss4[:, 8:12]), reads=[ssB], acc=[ssB])
                P.op('dve', lambda e: e.tensor_tensor(out=hn.rearrange("p (g f) -> p g f", g=4),
                                                      in0=yfull.rearrange("p (g f) -> p g f", g=4),
                                                      in1=ss4[:, 12:16].unsqueeze(2).to_broadcast([128, 4, 512]), op=ALU.mult),
                     reads=[yfB, ssB], writes=[hnB])
                for half in range(2):
                    for c8 in range(8):
                        c = half * 8 + c8
                        P.op('pe', lambda e, c=c, c8=c8: e.transpose(out=Tb8[:, c8 * 128:(c8 + 1) * 128], in_=hn[:, c * 128:(c + 1) * 128],
                                                                     identity=ident_b),
                             reads=[hnB, cbuf], writes=[bankbuf[7]] if c8 == 0 else [], acc=[] if c8 == 0 else [bankbuf[7]])
                    dst = ynst[ysl][:, half * 8:(half + 1) * 8, tsl]
                    src = Tb8.rearrange("p (c t) -> p c t", c=8)
                    first = (sub == 0 and half == 0)
                    if half == 0:
                        P.op('act', lambda e, dst=dst, src=src: e.activation(out=dst, in_=src, func=AF.Copy),
                             reads=[bankbuf[7]] + ([] if first else [ynB[ysl]]), writes=[ynB[ysl]] if first else [], acc=[] if first else [ynB[ysl]])
                    else:
                        P.op('dve', lambda e, dst=dst, src=src: e.tensor_copy(out=dst, in_=src),
                             reads=[bankbuf[7], ynB[ysl]], acc=[ynB[ysl]])

            def do_group(blk, sub, sl, g, tsl, dts):
                P.op('pe', lambda e: e.matmul(banks[1][:, 0:128], lhsT=XC[:, 16 + g, tsl], rhs=XC[:, 20 + g, tsl], start=True, stop=True),
                     reads=[XCb], writes=[bankbuf[1]])
                P.op('dve', lambda e: e.tensor_tensor(out=CBm, in0=banks[1][:, 0:128], in1=U_sb, op=ALU.mult),
                     reads=[bankbuf[1], cbuf], writes=[CBb])
                for h4 in range(2):
                    bk = 2 + h4
                    for q in range(4):
                        hh = h4 * 4 + q
                        h = g * 8 + hh
                        P.op('pool', lambda e, q=q, h=h: e.tensor_scalar(LH[q], LST_sb, dtA[:, h:h + 1], 1.0, ALU.mult, ALU.mult),
                             reads=[smB, cbuf], writes=[LHb[q]])
                        P.op('pe', lambda e, q=q, bk=bk: e.matmul(banks[bk][:, q * 128:(q + 1) * 128], lhsT=LH[q], rhs=U_sb, start=True, stop=True),
                             reads=[LHb[q], cbuf], writes=[bankbuf[bk]] if q == 0 else [], acc=[] if q == 0 else [bankbuf[bk]])
                    P.op('act', lambda e, bk=bk, h4=h4: e.activation(out=ED[h4], in_=banks[bk][:, :], func=AF.Exp),
                         reads=[bankbuf[bk]], writes=[EDb[h4]])
                    for q in range(4):
                        hh = h4 * 4 + q
                        h = g * 8 + hh
                        P.op('dve', lambda e, q=q, hh=hh, h=h, h4=h4: e.scalar_tensor_tensor(
                            out=WT[hh], in0=CBm, scalar=dts[:, h:h + 1], in1=ED[h4][:, q * 128:(q + 1) * 128], op0=ALU.mult, op1=ALU.mult),
                            reads=[CBb, EDb[h4], inB[sl]], writes=[WTb[hh]])
                gs = slice(g * 512, (g + 1) * 512)
                P.op('pe', lambda e: e.matmul(banks[4][:, :], lhsT=ident_b, rhs=xD[:, gs], start=True, stop=False, skip_group_check=True),
                     reads=[xdB, cbuf], writes=[bankbuf[4]])
                for hh in range(8):
                    h = g * 8 + hh
                    P.op('pe', lambda e, hh=hh, h=h: e.matmul(banks[4][:, hh * 64:(hh + 1) * 64], lhsT=WT[hh],
                                                             rhs=XS[:, sub, h * 64:(h + 1) * 64], start=False, stop=False, skip_group_check=True),
                         reads=[WTb[hh], XSb], acc=[bankbuf[4]])
                P.op('pe', lambda e: e.matmul(banks[5][:, :], lhsT=XC[:, 20 + g, tsl], rhs=stbf[:, gs], start=True, stop=True),
                     reads=[XCb, stbB], writes=[bankbuf[5]])
                P.op('act', lambda e: e.activation(out=Ysb, in_=banks[4][:, :], func=AF.Copy), reads=[bankbuf[4]], writes=[YsB])
                P.op('dve', lambda e: e.tensor_tensor(out=ytmp.rearrange("p (h d) -> p h d", h=8),
                                                      in0=banks[5][:, :].rearrange("p (h d) -> p h d", h=8),
                                                      in1=ecs[:, g * 8:(g + 1) * 8].unsqueeze(2).to_broadcast([128, 8, 64]), op=ALU.mult),
                     reads=[bankbuf[5], smB], writes=[ytB])
                P.op('pool', lambda e: e.tensor_tensor(out=yfull[:, gs], in0=ytmp, in1=Ysb, op=ALU.add),
                     reads=[ytB, YsB], writes=[yfB] if g == 0 else [], acc=[] if g == 0 else [yfB])
                P.op('pe', lambda e: e.matmul(banks[6][:, :], lhsT=BT[:, sub, g * 128:(g + 1) * 128], rhs=xw[:, gs], start=True, stop=True),
                     reads=[BTb, xdB], writes=[bankbuf[6]])
                P.op('pool', lambda e: e.tensor_tensor(out=stmp.rearrange("p (h d) -> p h d", h=8),
                                                       in0=st32[:, gs].rearrange("p (h d) -> p h d", h=8),
                                                       in1=ecs[:, 32 + g * 8:32 + (g + 1) * 8].unsqueeze(2).to_broadcast([128, 8, 64]), op=ALU.mult),
                     reads=[stB, smB], writes=[stmpB])
                P.op('dve', lambda e: e.tensor_tensor(out=st32[:, gs], in0=stmp, in1=banks[6][:, :], op=ALU.add),
                     reads=[stmpB, bankbuf[6], stB], acc=[stB])
                P.op('act', lambda e: e.activation(out=stbf[:, gs], in_=st32[:, gs], func=AF.Copy), reads=[stB, stbB], acc=[stbB])

            def do_block(blk):
                sl = 0
                xp = XP[sl]
                P.dma('sp', chan(f'xp{sl}'), xp[:, :, 3:515], xbcT_d[b, :, blk * 512:(blk + 1) * 512].rearrange("(c p) t -> p c t", p=128),
                      reads=[scr[b], XPb[sl]], acc=[XPb[sl]])
                P.dma('sp', chan(f'si{sl}'), DT[sl], dt_d[b, blk * 512:(blk + 1) * 512, :].rearrange("(s p) h -> p s h", p=128),
                      reads=[scr[b]], writes=[inB[sl]])
                P.dma('sp', chan(f'si{sl}'), SZS[sl], szs_d[b, blk * 512:(blk + 1) * 512, :].rearrange("(s p) f -> p s f", p=128),
                      reads=[scr[b], inB[sl]], acc=[inB[sl]])
                for c in range(24):
                    a = c % 2
                    ac = acc32[a]
                    P.op('dve', lambda e, c=c, ac=ac: e.tensor_scalar(ac, xp[:, c, 0:512], convw_sb[:, c, 0:1], convb_sb[:, c:c + 1],
                                                                     ALU.mult, ALU.add),
                         reads=[XPb[sl], cbuf], writes=[accB[a]])
                    for k in range(1, 4):
                        P.op('dve', lambda e, c=c, k=k, ac=ac: e.scalar_tensor_tensor(
                            out=ac, in0=xp[:, c, k:k + 512], scalar=convw_sb[:, c, k:k + 1], in1=ac, op0=ALU.mult, op1=ALU.add),
                            reads=[XPb[sl], cbuf, accB[a]], acc=[accB[a]])
                    P.op('act', lambda e, c=c, ac=ac: e.activation(out=XC[:, c, :], in_=ac, func=AF.Silu),
                         reads=[accB[a]] + ([XCb] if c else []), writes=[XCb] if c == 0 else [], acc=[] if c == 0 else [XCb])
                P.op('pool', lambda e: e.tensor_copy(out=xp[:, :, 0:3], in_=xp[:, :, 512:515]), reads=[XPb[sl]], acc=[XPb[sl]])
                for sub in range(4):
                    tsl = slice(sub * 128, (sub + 1) * 128)
                    for half in range(2):
                        for c8 in range(8):
                            c = half * 8 + c8
                            P.op('pe', lambda e, c=c, c8=c8, tsl=tsl: e.transpose(out=Tb8[:, c8 * 128:(c8 + 1) * 128], in_=XC[:, c, tsl],
                                                                               identity=ident_b),
                                 reads=[XCb, cbuf], writes=[bankbuf[7]] if c8 == 0 else [], acc=[] if c8 == 0 else [bankbuf[7]])
                        first = (sub == 0 and half == 0)
                        eng = 'act' if half == 0 else 'dve'
                        cp(eng, XS[:, sub, half * 1024:(half + 1) * 1024], Tb8[:, :], reads=[bankbuf[7]] + ([] if first else [XSb]),
                           writes=[XSb] if first else [], acc=[] if first else [XSb])
                    for gq in range(4):
                        P.op('pe', lambda e, gq=gq, tsl=tsl: e.transpose(out=Tb8[:, gq * 128:(gq + 1) * 128], in_=XC[:, 16 + gq, tsl],
                                                                       identity=ident_b),
                             reads=[XCb, cbuf], writes=[bankbuf[7]] if gq == 0 else [], acc=[] if gq == 0 else [bankbuf[7]])
                    cp('act', BT[:, sub, :], Tb8[:, 0:512], reads=[bankbuf[7]] + ([] if sub == 0 else [BTb]),
                       writes=[BTb] if sub == 0 else [], acc=[] if sub == 0 else [BTb])
                for sub in range(4):
                    do_chunk(blk, sub, sl, sl)
                P.dma('pool', chan(f'yn{sl}'), ynT_d[b, :, blk * 512:(blk + 1) * 512].rearrange("(c p) t -> p c t", p=128),
                      ynst[sl], reads=[ynB[sl]], acc=[scr2[b]])

            for blk in range(NBLK):
                do_block(blk)
            P.barrier()
            A.release(m0)

        def do_out(b):
            m0 = A.mark()
            Wn = A.alloc(8 * 1024, BF).rearrange("p (k c) -> p k c", k=8)
            Ws = A.alloc(16 * 1024, BF).rearrange("p (k c) -> p k c", k=16)
            Wo = A.alloc(8 * 1024, BF).rearrange("p (k c) -> p k c", k=8)
            load_out_weights(Wn, Ws, Wo)
            OZ = [A.alloc(8 * 512, BF).rearrange("p (k t) -> p k t", k=8)] * 2
            YN = [A.alloc(16 * 512, BF).rearrange("p (k t) -> p k t", k=16)] * 2
            GT = [A.alloc(16 * 512, BF).rearrange("p (k t) -> p k t", k=16)] * 2
            X = [A.alloc(4 * 1024, F32).rearrange("p (s d) -> p s d", s=4)] * 2
            _ld = Buf('ld0')
            ldB = [_ld, _ld]
            mT = A.alloc(8 * 512, BF).rearrange("p (k t) -> p k t", k=8)
            mTb = Buf('mT')
            t1 = [A.alloc(512, F32) for _ in range(2)]
            t2 = [A.alloc(512, F32) for _ in range(2)]
            t1B = [Buf('t1_0'), Buf('t1_1')]
            t2B = [Buf('t2_0'), Buf('t2_1')]
            R = [A.alloc(1024, F32) for _ in range(2)]
            RB = [Buf('R0'), Buf('R1')]
            OS = [A.alloc(1024, F32) for _ in range(2)]
            OSB = [Buf('OS0'), Buf('OS1')]
            sq = A.alloc(1024, BF)
            sqB = Buf('sq')
            s4 = [A.alloc(4, F32) for _ in range(2)]
            s4B = [Buf('s4_0'), Buf('s4_1')]

            def do_blk(blk):
                sl = 0
                tsl = slice(blk * 512, (blk + 1) * 512)
                P.dma('sp', chan(f'oa{sl}'), OZ[sl], ozT_d[b, :, tsl].rearrange("(k p) t -> p k t", p=128), reads=[scr2[b]], writes=[ldB[sl]])
                P.dma('sp', chan(f'ob{sl}'), YN[sl], ynT_d[b, :, tsl].rearrange("(k p) t -> p k t", p=128), reads=[scr2[b], ldB[sl]], acc=[ldB[sl]])
                P.dma('sp', chan(f'oc{sl}'), GT[sl], gT_d[b, :, tsl].rearrange("(k p) t -> p k t", p=128), reads=[scr[b], ldB[sl]], acc=[ldB[sl]])
                P.dma('sp', chan(f'od{sl}'), X[sl], x_d[b, tsl, :].rearrange("(s p) d -> p s d", p=128), reads=[ldB[sl]], acc=[ldB[sl]])
                for fo in range(8):
                    a = fo % 2
                    fsl = slice(fo * 128, (fo + 1) * 128)
                    for k in range(8):
                        P.op('pe', lambda e, k=k, fsl=fsl, a=a: e.matmul(banks[a][:, :], lhsT=Wn[:, k, fsl], rhs=OZ[sl][:, k, :],
                                                                        start=(k == 0), stop=(k == 7)),
                             reads=[wobuf, ldB[sl]], writes=[bankbuf[a]] if k == 0 else [], acc=[] if k == 0 else [bankbuf[a]])
                    for k in range(16):
                        P.op('pe', lambda e, k=k, fsl=fsl, a=a: e.matmul(banks[2 + a][:, :], lhsT=Ws[:, k, fsl], rhs=YN[sl][:, k, :],
                                                                        start=(k == 0), stop=(k == 15)),
                             reads=[wobuf, ldB[sl]], writes=[bankbuf[2 + a]] if k == 0 else [], acc=[] if k == 0 else [bankbuf[2 + a]])
                    P.op('dve', lambda e, fo=fo, a=a: e.tensor_tensor(out=t1[a], in0=banks[a][:, :], in1=GT[sl][:, fo, :], op=ALU.mult),
                         reads=[bankbuf[a], ldB[sl]], writes=[t1B[a]])
                    P.op('dve', lambda e, fo=fo, a=a: e.tensor_tensor(out=t2[a], in0=banks[2 + a][:, :], in1=GT[sl][:, 8 + fo, :], op=ALU.mult),
                         reads=[bankbuf[2 + a], ldB[sl]], writes=[t2B[a]])
                    P.op('pool', lambda e, fo=fo, a=a: e.tensor_tensor(out=mT[:, fo, :], in0=t1[a], in1=t2[a], op=ALU.add),
                         reads=[t1B[a], t2B[a]] + ([mTb] if fo else []), writes=[mTb] if fo == 0 else [], acc=[] if fo == 0 else [mTb])
                for sub in range(4):
                    r = sub % 2
                    for cb in range(2):
                        bk = 4 + (sub * 2 + cb) % 4
                        csl = slice(cb * 512, (cb + 1) * 512)
                        for k in range(8):
                            P.op('pe', lambda e, k=k, bk=bk, csl=csl, sub=sub: e.matmul(
                                banks[bk][:, :], lhsT=mT[:, k, sub * 128:(sub + 1) * 128], rhs=Wo[:, k, csl], start=(k == 0), stop=(k == 7)),
                                reads=[mTb, wobuf], writes=[bankbuf[bk]] if k == 0 else [], acc=[] if k == 0 else [bankbuf[bk]])
                        P.op('dve', lambda e, bk=bk, csl=csl, sub=sub, r=r: e.tensor_tensor(out=R[r][:, csl], in0=banks[bk][:, :],
                                                                                         in1=X[sl][:, sub, csl], op=ALU.add),
                             reads=[bankbuf[bk], ldB[sl]] + ([RB[r]] if cb else []), writes=[RB[r]] if cb == 0 else [], acc=[] if cb == 0 else [RB[r]])
                    P.op('act', lambda e, r=r: e.activation(out=sq, in_=R[r], func=AF.Square, accum_out=s4[r][:, 0:1]),
                         reads=[RB[r]], writes=[sqB, s4B[r]])
                    P.op('dve', lambda e, r=r: e.tensor_scalar(s4[r][:, 1:2], s4[r][:, 0:1], 1.0 / D, EPS, ALU.mult, ALU.add),
                         reads=[s4B[r]], acc=[s4B[r]])
                    P.op('act', lambda e, r=r: e.activation(out=s4[r][:, 2:3], in_=s4[r][:, 1:2], func=AF.Sqrt), reads=[s4B[r]], acc=[s4B[r]])
                    P.op('dve', lambda e, r=r: e.reciprocal(s4[r][:, 3:4], s4[r][:, 2:3]), reads=[s4B[r]], acc=[s4B[r]])
                    P.op('dve', lambda e, r=r: e.scalar_tensor_tensor(out=OS[r], in0=R[r], scalar=s4[r][:, 3:4], in1=fnw_bc,
                                                                      op0=ALU.mult, op1=ALU.mult),
                         reads=[RB[r], s4B[r], cbuf], writes=[OSB[r]])
                    if 'out_nostore' not in debug:
                        P.dma('sp', chan(f'out{r}'),
                              out_d[b, blk * 512 + sub * 128:blk * 512 + (sub + 1) * 128, :], OS[r],
                              reads=[OSB[r]], acc=[outbuf])

            for blk in range(S // 512):
                do_blk(blk)
            P.barrier()
            A.release(m0)

        outbuf = Buf('out')

        scr = [Buf(f'scratch{i}') for i in range(NB)]
        scr2 = [Buf(f'scratch2_{i}') for i in range(NB)]
        for b in range(NB):
            seq_mark = A.mark()
            xnT = A.alloc(8 * S, BF).rearrange("p (k s) -> p k s", k=8)
            xnb = [Buf(f'xnT{m}') for m in range(NTT)]
            m1 = A.mark()
            xt = [A.alloc(1024, F32) for _ in range(3)]
            xtb = [Buf(f'xt{i}') for i in range(3)]
            xch = [P.new_chan() for _ in range(3)] if b == 0 else xch
            junk = A.alloc(1024, BF)
            junkb = Buf('junk')
            st4 = [A.alloc(4, F32) for _ in range(3)]
            st4b = [Buf('st4') for _ in range(3)]
            for m in range(NTT):
                sl = m % 3
                P.dma('sp', xch[sl], xt[sl], x_d[b, m * 128:(m + 1) * 128, :], writes=[xtb[sl]])
                s4 = st4[sl]
                P.op('act', lambda e, sl=sl, s4=s4: e.activation(out=junk, in_=xt[sl], func=AF.Square, accum_out=s4[:, 0:1]),
                     reads=[xtb[sl]], writes=[junkb, st4b[sl]])
                P.op('dve', lambda e, s4=s4: e.tensor_scalar(s4[:, 1:2], s4[:, 0:1], 1.0 / D, EPS, ALU.mult, ALU.add),
                     reads=[st4b[sl]], acc=[st4b[sl]])
                P.op('act', lambda e, s4=s4: e.activation(out=s4[:, 2:3], in_=s4[:, 1:2], func=AF.Sqrt),
                     reads=[st4b[sl]], acc=[st4b[sl]])
                P.op('dve', lambda e, s4=s4: e.reciprocal(s4[:, 3:4], s4[:, 2:3]), reads=[st4b[sl]], acc=[st4b[sl]])
                P.op('act', lambda e, sl=sl, s4=s4: e.activation(out=xt[sl], in_=xt[sl], func=AF.Copy, scale=s4[:, 3:4]),
                     reads=[st4b[sl], xtb[sl]], acc=[xtb[sl]])
                for half in range(2):
                    bk = (2 * m + half) % 4
                    for kk in range(4):
                        kc = half * 4 + kk
                        P.op('pe', lambda e, bk=bk, kk=kk, kc=kc, sl=sl: e.transpose(
                            out=banks[bk][:, kk * 128:(kk + 1) * 128], in_=xt[sl][:, kc * 128:(kc + 1) * 128], identity=ident_f),
                            reads=[xtb[sl], cbuf], writes=[bankbuf[bk]] if kk == 0 else [], acc=[] if kk == 0 else [bankbuf[bk]])
                    eng = 'dve' if half == 0 else 'act'
                    dst = xnT[:, half * 4:half * 4 + 4, m * 128:(m + 1) * 128]
                    src = banks[bk][:, :].rearrange("p (k t) -> p k t", k=4)
                    if eng == 'dve':
                        P.op('dve', lambda e, dst=dst, src=src: e.tensor_copy(out=dst, in_=src),
                             reads=[bankbuf[bk]], acc=[xnb[m]])
                    else:
                        P.op('act', lambda e, dst=dst, src=src: e.activation(out=dst, in_=src, func=AF.Copy),
                             reads=[bankbuf[bk]], acc=[xnb[m]])
            if 'dbg_xnT' in debug and b == 0:
                dbg_x = nc.dram_tensor('dbg_xnT', [128, 8, S], BF, kind="ExternalOutput").ap()
                P.dma('sp', next_cc(), dbg_x, xnT, reads=xnb)
            if stop_after <= 1:
                break

            W32 = [A.alloc(8 * 512, F32).rearrange("p (k c) -> p k c", k=8) for _ in range(2)]
            W32b = [Buf('w32_0'), Buf('w32_1')]
            Wb = [A.alloc(8 * 512, BF).rearrange("p (k c) -> p k c", k=8) for _ in range(2)]
            Wbb = [Buf('wb0'), Buf('wb1')]
            stF = [A.alloc(S, BF) for _ in range(3)]
            stFb = [Buf(f'stF{i}') for i in range(3)]
            stT = [A.alloc(4 * 512, F32) for _ in range(2)]
            stTb = [Buf(f'stT{i}') for i in range(2)]
            tmp32 = A.alloc(64, F32)
            tmp32b = Buf('tmp32')
            if b == 0:
                wch = [P.new_chan(), P.new_chan()]
                fch = [P.new_chan() for _ in range(3)]
                tch = [P.new_chan() for _ in range(2)]
            scrbuf = scr[b]
            w_v = w_in_d.rearrange("(k p) c -> p k c", p=128)
            blocks = []

            def fm(c0, n, func, scale, dest, r0):
                for cc in range(0, n, 512):
                    nn = min(512, n - cc)
                    blocks.append(('F', c0 + cc, nn, func, scale, dest, r0 + cc))

            def tm(c0, n, func, dest, dc0, dt):
                for cc in range(0, n, 512):
                    nn = min(512, n - cc)
                    blocks.append(('T', c0 + cc, nn, func, 1.0, dest, dc0 + cc, dt))

            fm(C_Q, 1024, 'copy', 0.125, qT_d, 0)
            fm(C_KC, 256, 'copy', 1.0, kcT_d, 0)
            fm(C_VC, 256, 'copy', 1.0, vcT_d, 0)
            fm(C_KS, 256, 'copy', 1.0, ksT_d, 0)
            tm(C_VS, 256, 'copy', vs_d, 0, BF)
            fm(C_KW, 256, 'copy', 1.0, kwT_d, 0)
            tm(C_VW, 256, 'copy', vw_d, 0, BF)
            tm(C_GATE, 48, 'sigmoid', gate_d, 0, F32)
            fm(C_ZN, 1024, 'silu', 1.0, szT_d, 0)
            tm(C_ZS, 2048, 'silu', szs_d, 0, BF)
            fm(C_XBC, 3072, 'copy', 1.0, xbcT_d, 0)
            tm(C_DT, 32, 'softplus', dt_d, 0, F32)
            fm(C_MG, 2048, 'sigmoid', 1.0, gT_d, 0)

            bkc = [0]
            fcnt = [0]
            tcnt = [0]
            evc = [0]

            def evac(func, scale, dst, src, reads, writes=(), acc=()):
                if func == 'copy':
                    evc[0] += 1
                    if evc[0] % 2 == 0:
                        P.op('dve', lambda e: e.tensor_scalar(dst, src, float(scale), None, ALU.mult),
                             reads=reads, writes=writes, acc=acc)
                    else:
                        P.op('act', lambda e: e.activation(out=dst, in_=src, func=AF.Copy, scale=float(scale)),
                             reads=reads, writes=writes, acc=acc)
                elif func == 'silu':
                    P.op('act', lambda e: e.activation(out=dst, in_=src, func=AF.Silu), reads=reads, writes=writes, acc=acc)
                elif func == 'sigmoid':
                    P.op('act', lambda e: e.activation(out=dst, in_=src, func=AF.Sigmoid), reads=reads, writes=writes, acc=acc)
                else:
                    raise ValueError(func)

            for bi, blk in enumerate(blocks):
                kind, c0, n = blk[0], blk[1], blk[2]
                sl = bi % 2
                P.dma('sp', wch[sl], W32[sl][:, :, 0:n], w_v[:, :, c0:c0 + n], writes=[W32b[sl]])
                P.op('dve', lambda e, sl=sl, n=n: e.tensor_tensor(
                    out=Wb[sl][:, :, 0:n], in0=W32[sl][:, :, 0:n],
                    in1=normw_sb[:, 0:8].unsqueeze(2).to_broadcast([128, 8, n]), op=ALU.mult),
                    reads=[W32b[sl], cbuf], writes=[Wbb[sl]])
                if kind == 'F':
                    _, _, _, func, scale, dest, r0 = blk
                    for j in range(n // 128):
                        fs = fcnt[0] % 3
                        fcnt[0] += 1
                        for tt in range(S // 512):
                            bk = bkc[0] % 4
                            bkc[0] += 1
                            for kc in range(8):
                                P.op('pe', lambda e, bk=bk, sl=sl, j=j, kc=kc, tt=tt: e.matmul(
                                    banks[bk][:, :], lhsT=Wb[sl][:, kc, j * 128:(j + 1) * 128],
                                    rhs=xnT[:, kc, tt * 512:(tt + 1) * 512], start=(kc == 0), stop=(kc == 7)),
                                    reads=[Wbb[sl]] + xnb[tt * 4:tt * 4 + 4], writes=[bankbuf[bk]] if kc == 0 else [],
                                    acc=[] if kc == 0 else [bankbuf[bk]])
                            evac(func, scale, stF[fs][:, tt * 512:(tt + 1) * 512], banks[bk][:, :], [bankbuf[bk]],
                                 writes=[stFb[fs]] if tt == 0 else [], acc=[] if tt == 0 else [stFb[fs]])
                        P.dma('pool', fch[fs], dest[b, r0 + j * 128:r0 + (j + 1) * 128, :], stF[fs][:, 0:S],
                              reads=[stFb[fs]], acc=[scrbuf])
                else:
                    _, _, _, func, scale, dest, dc0, ddt = blk
                    for mg in range(NTT // 4):
                        ts_ = tcnt[0] % 2
                        tcnt[0] += 1
                        stv = stT[ts_] if ddt == F32 else stT[ts_].bitcast(BF)
                        stv = stv[:, 0:4 * n].rearrange("p (m c) -> p m c", m=4)
                        for mm in range(4):
                            m = mg * 4 + mm
                            bk = bkc[0] % 4
                            bkc[0] += 1
                            for kc in range(8):
                                P.op('pe', lambda e, bk=bk, sl=sl, kc=kc, m=m, n=n: e.matmul(
                                    banks[bk][:, 0:n], lhsT=xnT[:, kc, m * 128:(m + 1) * 128],
                                    rhs=Wb[sl][:, kc, 0:n], start=(kc == 0), stop=(kc == 7)),
                                    reads=[Wbb[sl], xnb[m]], writes=[bankbuf[bk]] if kc == 0 else [],
                                    acc=[] if kc == 0 else [bankbuf[bk]])
                            wr = dict(writes=[stTb[ts_]] if mm == 0 else [], acc=[] if mm == 0 else [stTb[ts_]])
                            if func == 'softplus':
                                P.op('dve', lambda e, bk=bk, n=n: e.tensor_tensor(out=tmp32[:, 0:n], in0=banks[bk][:, 0:n],
                                                                                   in1=dtb_bc[:, 0:n], op=ALU.add),
                                     reads=[bankbuf[bk], cbuf], writes=[tmp32b])
                                P.op('act', lambda e, n=n: e.activation(out=tmp32[:, 0:n], in_=tmp32[:, 0:n], func=AF.Exp),
                                     reads=[tmp32b], acc=[tmp32b])
                                P.op('act', lambda e, n=n, stv=stv, mm=mm: e.activation(out=stv[:, mm, :], in_=tmp32[:, 0:n],
                                                                                          func=AF.Ln, bias=1.0),
                                     reads=[tmp32b], **wr)
                            else:
                                evac(func, scale, stv[:, mm, :], banks[bk][:, 0:n], [bankbuf[bk]], **wr)
                        P.dma('pool', tch[ts_],
                              dest[b, mg * 512:(mg + 1) * 512, dc0:dc0 + n].rearrange("(m p) c -> p m c", p=128),
                              stv, reads=[stTb[ts_]], acc=[scrbuf])
            A.release(seq_mark)
            P.barrier()
            if stop_after <= 2:
                continue
            do_nsa(b)
            if stop_after <= 4:
                continue
            do_ssd(b)
            if stop_after <= 5:
                continue
            do_out(b)

        P.barrier()
        P.op('sp', lambda e: e.nop(), reads=[], writes=[])
        P.op('act', lambda e: e.activation(out=tinyc[:, 0:1], in_=tinyc[:, 1:2], func=AF.Copy), reads=[cbuf], writes=[])

        @block.tensor
        def _(e):
            P.emit('pe', e, esems, csems)

        @block.scalar
        def _(e):
            P.emit('act', e, esems, csems)

        @block.vector
        def _(e):
            P.emit('dve', e, esems, csems)

        @block.gpsimd
        def _(e):
            P.emit('pool', e, esems, csems)

        @block.sync
        def _(e):
            P.emit('sp', e, esems, csems)

    return nc, hc


def phase_cmp_nsa(*a):
    raise NotImplementedError


def phase_ssd(*a):
    raise NotImplementedError


def phase_out(*a):
    raise NotImplementedError


def make_in_maps(inputs, hc, NB, ncores):
    f = lambda a: np.ascontiguousarray(np.asarray(a, dtype=np.float32))
    x = f(inputs['x'])
    shared = {
        'w_in': f(inputs['w_in'][0]),
        'norm_w': f(inputs['norm_w'][0].reshape(8, 128).T),
        'posT_k': f(inputs['cmp_pos_k'][0].T),
        'posT_v': f(inputs['cmp_pos_v'][0].T),
        'w1_k': f(inputs['cmp_k_w1'][0].reshape(32, 64, 256).transpose(1, 0, 2)),
        'w1_v': f(inputs['cmp_v_w1'][0].reshape(32, 64, 256).transpose(1, 0, 2)),
        'b1_k': f(inputs['cmp_k_b1'][0].reshape(2, 128).T),
        'b1_v': f(inputs['cmp_v_b1'][0].reshape(2, 128).T),
        'w2_k': f(inputs['cmp_k_w2'][0].reshape(2, 128, 64).transpose(1, 0, 2)),
        'w2_v': f(inputs['cmp_v_w2'][0].reshape(2, 128, 64).transpose(1, 0, 2)),
        'conv_w': f(inputs['conv_w'][0].reshape(4, 24, 128).transpose(2, 1, 0)),
        'conv_b': f(inputs['conv_b'][0].reshape(24, 128).T),
        'dt_bias': f(inputs['dt_bias'][0].reshape(1, 32)),
        'a_log': f(inputs['a_log'][0].reshape(1, 32)),
        'd_skip': f(inputs['d_skip'][0].reshape(1, 32)),
        'ssm_norm_w': f(inputs['ssm_norm_w'][0].reshape(16, 128).T),
        'w_out_nsa': f(inputs['w_out_nsa'][0]),
        'w_out_ssm': f(inputs['w_out_ssm'][0]),
        'w_out': f(inputs['w_out'][0]),
        'rel_bias': f(inputs['rel_bias']),
        'final_norm_w': f(inputs['final_norm_w'].reshape(1, D)),
    }
    for nme in CONST_NAMES:
        shared['c_' + nme] = np.ascontiguousarray(hc[nme])
    maps = []
    for c in range(ncores):
        m = dict(shared)
        m['x'] = np.ascontiguousarray(x[c * NB:(c + 1) * NB])
        maps.append(m)
    return maps


_CACHE = {}


def kernel(**inputs):
    x = np.asarray(inputs['x'])
    B, S, _ = x.shape
    ncores = 8
    NB = B // ncores
    key = (S, NB)
    if key not in _CACHE:
        _CACHE[key] = build(S, NB)
    nc, hc = _CACHE[key]
    maps = make_in_maps(inputs, hc, NB, ncores)
    res = run_bass_kernel_spmd(nc, maps, core_ids=list(range(ncores)))
    out = np.concatenate([np.asarray(r['out']) for r in res.results], axis=0)
    return out.astype(np.float32)
```

```python
import math
import numpy as np
import ml_dtypes
import concourse.bass as bass
import concourse.mybir as mybir
from concourse.bass_utils import run_bass_kernel_spmd

F32 = mybir.dt.float32
BF = mybir.dt.bfloat16
AF = mybir.ActivationFunctionType
ALU = mybir.AluOpType
AX = mybir.AxisListType

D = 1024
NH = 16
NG = 4
DH = 64
NCOL = 10832
C_Q, C_KC, C_VC, C_KS, C_VS, C_KW, C_VW, C_GATE, C_ZN, C_ZS, C_XBC, C_DT, C_MG = (
    0, 1024, 1280, 1536, 1792, 2048, 2304, 2560, 2608, 3632, 5680, 8752, 8784)
EPS = 1e-6
NEG = -30000.0
ENGS = ['pe', 'act', 'dve', 'pool', 'sp']


class Buf:
    __slots__ = ('name', 'writers', 'readers')

    def __init__(self, name=''):
        self.name = name
        self.writers = []
        self.readers = []


class Chan:
    def __init__(self, idx):
        self.idx = idx
        self.count = 0
        self.last = None


class Op:
    __slots__ = ('eng', 'fn', 'waits', 'is_dma', 'chan', 'sigval')


class Prog:
    def __init__(self, n_chan):
        self.ops = {e: [] for e in ENGS}
        self.ncomp = {e: 0 for e in ENGS}
        self.known = {e: {} for e in ENGS}
        self.chans = [Chan(i) for i in range(n_chan)]
        self.last_comp = {e: None for e in ENGS}
        self.pending = {e: [] for e in ENGS}
        self.next_chan = 0

    def new_chan(self):
        c = self.chans[self.next_chan]
        self.next_chan += 1
        return c

    def barrier(self):
        deps = [o for o in self.last_comp.values() if o is not None]
        deps += [c.last for c in self.chans if c.last is not None]
        for e in ENGS:
            self.pending[e] = list(deps)

    def _record(self, eng, fn, reads, writes, acc, chan):
        op = Op()
        op.eng = eng
        op.fn = fn
        op.is_dma = chan is not None
        op.chan = chan
        deps = []
        if self.pending[eng]:
            deps += self.pending[eng]
            self.pending[eng] = []
        for r in reads:
            deps += r.writers
            r.readers.append(op)
        for w in writes:
            deps += w.writers
            deps += w.readers
            w.writers = [op]
            w.readers = []
        for w in acc:
            deps += w.readers
            if w.writers:
                deps.append(w.writers[0])
            w.readers = []
            w.writers.append(op)
        if chan is not None and chan.last is not None:
            deps.append(chan.last)
        waits = {}
        kn = self.known[eng]
        for y in deps:
            if y is op:
                continue
            if y.is_dma:
                key = ('c', y.chan.idx)
                val = y.sigval
            else:
                if y.eng == 'pe' and eng == 'pe' and not op.is_dma:
                    continue
                key = ('e', y.eng)
                val = y.sigval
            if val <= 0:
                continue
            if kn.get(key, 0) >= val:
                continue
            if waits.get(key, 0) < val:
                waits[key] = val
        if op.is_dma:
            chan.count += 1
            op.sigval = 16 * chan.count
            chan.last = op
        else:
            self.ncomp[eng] += 1
            op.sigval = self.ncomp[eng]
            self.last_comp[eng] = op
        for k, v in waits.items():
            kn[k] = v
        op.waits = list(waits.items())
        self.ops[eng].append(op)
        return op

    def op(self, eng, fn, reads=(), writes=(), acc=()):
        return self._record(eng, fn, reads, writes, acc, None)

    def dma(self, eng, chan, out, in_, reads=(), writes=(), acc=()):
        return self._record(eng, lambda e, o=out, i=in_: e.dma_start(out=o, in_=i), reads, writes, acc, chan)

    def emit(self, eng, e, esems, csems):
        for op in self.ops[eng]:
            for (kind, k), v in op.waits:
                e.wait_ge(esems[k] if kind == 'e' else csems[k], v)
            ins = op.fn(e)
            if op.is_dma:
                ins.then_inc(csems[op.chan.idx], 16)
            else:
                ins.then_inc(esems[eng], 1)


class Arena:
    def __init__(self, t, ncols):
        self.t = t
        self.n = ncols
        self.off = 0

    def alloc(self, cols, dt):
        w = cols if dt == F32 else (cols + 1) // 2
        a = self.t[:, self.off:self.off + w]
        self.off += w
        assert self.off <= self.n, ("SBUF arena overflow", self.off, self.n)
        return a if dt == F32 else a.bitcast(BF)

    def mark(self):
        return self.off

    def release(self, m):
        self.off = m


def t5_bucket_np(dist):
    import jax
    import jax.numpy as jnp
    with jax.default_device(jax.devices('cpu')[0]):
        d = jnp.maximum(jnp.asarray(dist, dtype=jnp.int32), 0)
        df = jnp.maximum(d, 1).astype(jnp.float32)
        large = 16 + (jnp.log(df / 16) / math.log(1024 / 16) * 16).astype(jnp.int32)
        out = jnp.where(d < 16, d, jnp.minimum(large, 31))
        return np.asarray(out)


def geometry(S):
    g = {}
    g['NQT'] = S // 512
    g['NKT'] = S // 128
    g['NCMP'] = S // 16 - 1
    g['NNT'] = max(1, (g['NCMP'] + 127) // 128)
    g['US'] = 1792
    g['UW'] = 1408
    minc = -31 - 2048 * (g['NNT'] - 1) - 16 * 127
    maxc = 512 * (g['NQT'] - 1) + 511 - 31
    g['MINV'] = min(minc, -511)
    g['MAXV'] = max(maxc, g['US'] - 1 - 384)
    g['LS'] = ((g['MAXV'] - g['MINV'] + 1 + 511) // 512) * 512
    g['LW'] = 1536
    return g


def host_consts(S):
    g = geometry(S)
    c = {}
    c['ident_f'] = np.eye(128, dtype=np.float32)
    c['ident_b'] = np.eye(128, dtype=np.float32).astype(ml_dtypes.bfloat16)
    s = np.arange(128)
    c['U'] = (s[:, None] <= s[None, :]).astype(np.float32)
    c['LST'] = (s[:, None] > s[None, :]).astype(np.float32)
    c['ONES'] = np.ones((128, 128), np.float32)
    E = (np.arange(S)[None, :] // 64 == np.arange(64)[:, None])
    c['Ec'] = E.astype(np.float32).astype(ml_dtypes.bfloat16)
    n = np.arange(g['NNT'] * 128)
    cs = n * 16
    ss = np.arange(64) * 64
    ov = ((cs[:, None] < ss[None, :] + 64) & (cs[:, None] + 32 > ss[None, :]) & (n[:, None] < g['NCMP']))
    c['OV'] = ov.astype(np.float32).reshape(g['NNT'], 128, 64).transpose(1, 0, 2).astype(ml_dtypes.bfloat16).copy()
    t = np.arange(S)
    cur = t // 64
    j = np.arange(64)
    valid = (j[None, :] * 64 <= t[:, None])
    forced = (j[None, :] == 0) | (j[None, :] == cur[:, None]) | (j[None, :] == cur[:, None] - 1)
    add = np.where(valid, np.where(forced, 1.0e4, 0.0), -1.0).astype(np.float32)
    c['ADDc'] = add.reshape(S // 128, 128, 64).transpose(1, 0, 2).copy()
    dS = np.arange(g['LS']) + g['MINV']
    bS = t5_bucket_np(dS)
    ohs = np.zeros((33, g['LS']), np.float32)
    ohs[bS, np.arange(g['LS'])] = 1.0
    ohs[:, dS < 0] = 0.0
    ohs[32, dS < 0] = 1.0
    c['OHS'] = ohs
    dW = np.arange(g['LW']) - 511
    bW = t5_bucket_np(dW)
    ohw = np.zeros((33, g['LW']), np.float32)
    ohw[bW, np.arange(g['LW'])] = 1.0
    bad = (dW < 0) | (dW >= 512)
    ohw[:, bad] = 0.0
    ohw[32, bad] = 1.0
    c['OHW'] = ohw
    assert np.all(t5_bucket_np(np.arange(897, 5000)) == 31)
    return c


CONST_NAMES = ['ident_f', 'ident_b', 'U', 'LST', 'ONES', 'Ec', 'OV', 'ADDc', 'OHS', 'OHW']
BF_CONSTS = {'ident_b', 'Ec', 'OV'}


def build(S, NB, debug=(), stop_after=99):
    geo = geometry(S)
    NQT, NKT, NCMP, NNT = geo['NQT'], geo['NKT'], geo['NCMP'], geo['NNT']
    US, UW, MINV, LS, LW = geo['US'], geo['UW'], geo['MINV'], geo['LS'], geo['LW']
    NTT = S // 128
    nc = bass.Bass("TRN2", target_bir_lowering=False)
    hc = host_consts(S)

    def din(name, shape, dt=F32):
        return nc.dram_tensor(name, list(shape), dt, kind="ExternalInput").ap()

    def dscr(name, shape, dt=BF):
        kind = "ExternalOutput" if name in debug else "Internal"
        return nc.dram_tensor(name, list(shape), dt, kind=kind).ap()

    x_d = din('x', [NB, S, D])
    out_d = nc.dram_tensor('out', [NB, S, D], F32, kind="ExternalOutput").ap()
    w_in_d = din('w_in', [D, NCOL])
    normw_d = din('norm_w', [128, 8])
    posk_d = din('posT_k', [64, 32])
    posv_d = din('posT_v', [64, 32])
    w1k_d = din('w1_k', [64, 32, 256])
    w1v_d = din('w1_v', [64, 32, 256])
    b1k_d = din('b1_k', [128, 2])
    b1v_d = din('b1_v', [128, 2])
    w2k_d = din('w2_k', [128, 2, 64])
    w2v_d = din('w2_v', [128, 2, 64])
    convw_d = din('conv_w', [128, 24, 4])
    convb_d = din('conv_b', [128, 24])
    dtb_d = din('dt_bias', [1, 32])
    alog_d = din('a_log', [1, 32])
    dsk_d = din('d_skip', [1, 32])
    snw_d = din('ssm_norm_w', [128, 16])
    won_d = din('w_out_nsa', [D, D])
    wos_d = din('w_out_ssm', [2 * D, D])
    wo_d = din('w_out', [D, D])
    relb_d = din('rel_bias', [32, 16])
    fnw_d = din('final_norm_w', [1, D])
    cd = {}
    for nme in CONST_NAMES:
        cd[nme] = din('c_' + nme, hc[nme].shape, BF if nme in BF_CONSTS else F32)

    qT_d = dscr('qT', [NB, 1024, S])
    kcT_d = dscr('kcT', [NB, 256, S])
    vcT_d = dscr('vcT', [NB, 256, S])
    ksT_d = dscr('ksT', [NB, 256, S])
    kwT_d = dscr('kwT', [NB, 256, S])
    vs_d = dscr('vs', [NB, S, 256])
    vw_d = dscr('vw', [NB, S, 256])
    gate_d = dscr('gate', [NB, S, 48], F32)
    szT_d = dscr('szT', [NB, 1024, S])
    szs_d = dscr('szs', [NB, S, 2048])
    xbcT_d = dscr('xbcT', [NB, 3072, S])
    dt_d = dscr('dtv', [NB, S, 32], F32)
    gT_d = dscr('gT', [NB, 2048, S])
    ozT_d = dscr('ozT', [NB, 1024, S])
    ynT_d = dscr('ynT', [NB, 2048, S])
    reps_d = dscr('reps', [16, 128, LS])
    repw_d = dscr('repw', [16, 128, LW])

    N_CHAN = 60
    P = Prog(N_CHAN)
    ARENA_COLS = 51 * 1024

    import contextlib
    with contextlib.ExitStack() as es:
        arena_t = es.enter_context(nc.sbuf_tensor("arena", [128, ARENA_COLS], F32))
        banks = [es.enter_context(nc.psum_tensor(f"bank{i}", [128, 512], F32)) for i in range(8)]
        esems = {e: es.enter_context(nc.semaphore("se_" + e)) for e in ENGS}
        csems = [es.enter_context(nc.semaphore(f"sc_{i}")) for i in range(N_CHAN)]
        block = es.enter_context(nc.Block())
        A = Arena(arena_t, ARENA_COLS)
        bankbuf = [Buf(f"bank{i}") for i in range(8)]

        ch_cs = [P.new_chan() for _ in range(4)]
        ch_ci = [0]

        class _RR:
            pass
        ch_c = None
        cbuf = Buf('consts')

        def next_cc():
            ch_ci[0] += 1
            return ch_cs[ch_ci[0] % 4]

        def load_const(ap_d, cols, dt, parts=128, shape3=None):
            t = A.alloc(cols, dt)
            dst = t[0:parts, :]
            src = ap_d
            if shape3 is not None:
                dst = dst.rearrange("p (a b) -> p a b", a=shape3[0])
            P.dma('sp', next_cc(), dst, src, acc=[cbuf])
            return dst

        ident_f = load_const(cd['ident_f'], 128, F32)
        ident_b = load_const(cd['ident_b'], 128, BF)
        U_sb = load_const(cd['U'], 128, F32)
        LST_sb = load_const(cd['LST'], 128, F32)
        ONES_sb = load_const(cd['ONES'], 128, F32)
        OV_sb = load_const(cd['OV'], NNT * 64, BF, shape3=(NNT, 64))
        ADD_sb = load_const(cd['ADDc'], NTT * 64, F32, shape3=(NTT, 64))
        normw_sb = load_const(normw_d, 8, F32)
        convw_sb = load_const(convw_d, 96, F32, shape3=(24, 4))
        convb_sb = load_const(convb_d, 24, F32)
        snw_sb = load_const(snw_d, 16, F32)
        b1k_sb = load_const(b1k_d, 2, F32)
        b1v_sb = load_const(b1v_d, 2, F32)
        dtb_bc = load_const(dtb_d[0:1, :].partition_broadcast(128), 32, F32)
        alog_bc = load_const(alog_d[0:1, :].partition_broadcast(128), 32, F32)
        dsk_bc = load_const(dsk_d[0:1, :].partition_broadcast(128), 32, F32)
        b31_bc = load_const(relb_d[31:32, :].partition_broadcast(128), 16, F32)
        fnw_bc = load_const(fnw_d[0:1, :].partition_broadcast(128), D, F32)
        relb_sb = A.alloc(16, F32)
        P.dma('sp', next_cc(), relb_sb[0:32, :], relb_d, acc=[cbuf])
        zero_b = A.alloc(512, BF)
        A_bc = A.alloc(32, F32)
        tinyc = A.alloc(2, F32)
        P.op('dve', lambda e: e.memset(relb_sb[32:33, :], NEG), acc=[cbuf])
        P.op('dve', lambda e: e.memset(zero_b, 0.0), acc=[cbuf])
        P.op('dve', lambda e: e.memset(tinyc, 0.0), acc=[cbuf])
        P.op('act', lambda e: e.activation(out=A_bc, in_=alog_bc, func=AF.Exp), reads=[cbuf], acc=[cbuf])
        P.op('dve', lambda e: e.tensor_scalar(A_bc, A_bc, -1.0, None, ALU.mult), reads=[cbuf], acc=[cbuf])
        if 'touch_top' in debug:
            P.op('dve', lambda e: e.memset(arena_t[:, ARENA_COLS - 6144:ARENA_COLS], 0.0), acc=[cbuf])
        persist_mark = A.mark()

        def setup_rep():
            m0 = A.mark()
            relrep = A.alloc(16 * 128, F32).rearrange("p (h c) -> p h c", h=16)
            tb = Buf('relrep')
            P.op('dve', lambda e: e.tensor_copy(out=relrep[0:33], in_=relb_sb[0:33, :].unsqueeze(2).to_broadcast([33, 16, 128])),
                 reads=[cbuf], writes=[tb])
            ch_oh = P.new_chan()
            ch_st = [P.new_chan(), P.new_chan()]
            for (oh_d, L, rep_d, nm) in ((cd['OHS'], LS, reps_d, 's'), (cd['OHW'], LW, repw_d, 'w')):
                oh = A.alloc(L, F32)
                ohb = Buf('oh')
                P.dma('sp', ch_oh, oh[0:33, :], oh_d, writes=[ohb])
                stg = [A.alloc(L, BF) for _ in range(2)]
                stb = [Buf('repst0'), Buf('repst1')]
                for h in range(16):
                    sl = h % 2
                    for cidx in range(L // 512):
                        bk = (h * (L // 512) + cidx) % 4
                        P.op('pe', lambda e, bk=bk, h=h, cidx=cidx, oh=oh: e.matmul(
                            banks[bk][:, :], lhsT=relrep[0:33, h, :], rhs=oh[0:33, cidx * 512:(cidx + 1) * 512],
                            start=True, stop=True), reads=[tb, ohb], writes=[bankbuf[bk]])
                        P.op('act', lambda e, bk=bk, sl=sl, cidx=cidx, stg=stg: e.activation(
                            out=stg[sl][:, cidx * 512:(cidx + 1) * 512], in_=banks[bk][:, :], func=AF.Exp),
                            reads=[bankbuf[bk]], writes=[stb[sl]] if cidx == 0 else [], acc=[] if cidx == 0 else [stb[sl]])
                    P.dma('sp', ch_st[sl], rep_d[h], stg[sl][:, 0:L], reads=[stb[sl]], acc=[repbuf])
            A.release(m0)

        repbuf = Buf('rep')
        setup_rep()
        P.barrier()

        wobuf = Buf('wout')
        wo_chs = [P.new_chan(), P.new_chan()]

        def load_out_weights(Wn, Ws, Wo):
            st = [A.alloc(2048, F32) for _ in range(2)]
            stb = [Buf('wst0'), Buf('wst1')]
            chs = wo_chs
            i = 0
            first = [True]
            for (wd, wsb, nk, scaled) in ((won_d, Wn, 8, False), (wos_d, Ws, 16, True), (wo_d, Wo, 8, False)):
                wv = wd.rearrange("(k p) c -> p k c", p=128)
                for k0 in range(0, nk, 2):
                    sl = i % 2
                    i += 1
                    s3 = st[sl].rearrange("p (k c) -> p k c", k=2)
                    P.dma('sp', chs[sl], s3, wv[:, k0:k0 + 2, :], writes=[stb[sl]])
                    wr = dict(writes=[wobuf]) if first[0] else dict(acc=[wobuf])
                    first[0] = False
                    if scaled:
                        P.op('pool', lambda e, s3=s3, k0=k0, wsb=wsb: e.tensor_tensor(
                            out=wsb[:, k0:k0 + 2, :], in0=s3, in1=snw_sb[:, k0:k0 + 2].unsqueeze(2).to_broadcast([128, 2, 1024]),
                            op=ALU.mult), reads=[stb[sl], cbuf], **wr)
                    else:
                        P.op('pool', lambda e, s3=s3, k0=k0, wsb=wsb: e.tensor_copy(out=wsb[:, k0:k0 + 2, :], in_=s3),
                             reads=[stb[sl]], **wr)


        CH = {}

        def chan(name):
            if name not in CH:
                CH[name] = P.new_chan()
            return CH[name]

        def cp(eng, dst, src, reads=(), writes=(), acc=()):
            if eng == 'act':
                P.op('act', lambda e: e.activation(out=dst, in_=src, func=AF.Copy), reads=reads, writes=writes, acc=acc)
            else:
                P.op(eng, lambda e: e.tensor_copy(out=dst, in_=src), reads=reads, writes=writes, acc=acc)

        def do_nsa(b):
            m0 = A.mark()
            NP = NNT * 128
            KcT = A.alloc(4 * NP, BF).rearrange("p (g n) -> p g n", g=4)
            Vc = A.alloc(4 * NNT * 65, BF).rearrange("p (g t c) -> p g t c", g=4, t=NNT)
            kcb = Buf('KcT')
            vcb = Buf('Vc')
            P.op('dve', lambda e: e.memset(KcT, 0.0), writes=[kcb])
            P.op('pool', lambda e: e.memset(Vc, 1.0), writes=[vcb])
            mc = A.mark()
            KC = A.alloc(4 * S, BF).rearrange("p (g s) -> p g s", g=4)
            KCb = Buf('KC')
            w1f = A.alloc(32 * 256, F32)
            w1fb = Buf('w1f')
            w1 = A.alloc(32 * 256, BF).rearrange("p (l c) -> p l c", l=32)
            w1b = Buf('w1')
            posf = A.alloc(34, F32)
            posb_ = A.alloc(34, BF)
            posB = Buf('pos')
            w2f = A.alloc(128, F32)
            w2 = A.alloc(128, BF).rearrange("p (c d) -> p c d", c=2)
            w2B = Buf('w2')
            bias_sb = A.alloc(2, F32)
            biasB = Buf('bias')
            hT = A.alloc(2 * 4 * NP, BF).rearrange("p (c g n) -> p c g n", c=2, g=4)
            hTb = Buf('hT')
            for kind in ('k', 'v'):
                src_d, w1_d, pos_d, b1_sb, w2_d = ((kcT_d, w1k_d, posk_d, b1k_sb, w2k_d) if kind == 'k'
                                                   else (vcT_d, w1v_d, posv_d, b1v_sb, w2v_d))
                P.dma('sp', chan('kc'), KC[0:64], src_d[b].rearrange("(g d) s -> d g s", d=64), reads=[scr[b]], writes=[KCb])
                P.dma('sp', chan('w1'), w1f[0:64, :], w1_d.rearrange("d l c -> d (l c)"), writes=[w1fb])
                P.op('pool', lambda e: e.tensor_copy(out=w1[0:64].rearrange("p l c -> p (l c)"), in_=w1f[0:64, :]),
                     reads=[w1fb], writes=[w1b])
                P.op('dve', lambda e: e.memset(posf[0:64, :], 0.0), writes=[posB])
                P.dma('sp', chan('w1'), posf[0:64, 0:32], pos_d, reads=[posB], acc=[posB])
                P.op('dve', lambda e: e.tensor_copy(out=posb_[0:64, :], in_=posf[0:64, :]), reads=[posB], acc=[posB])
                P.dma('sp', chan('w1'), w2f, w2_d.rearrange("p c d -> p (c d)"), writes=[w2B])
                P.op('dve', lambda e: e.tensor_copy(out=w2.rearrange("p c d -> p (c d)"), in_=w2f), reads=[w2B], acc=[w2B])
                P.op('pool', lambda e: e.memset(hT, 0.0), writes=[hTb])
                for c in range(2):
                    for l in range(32):
                        P.op('pe', lambda e, c=c, l=l: e.matmul(banks[7][:, 0:2], lhsT=w1[0:64, l, c * 128:(c + 1) * 128],
                                                               rhs=posb_[0:64, l:l + 2], start=(l == 0), stop=(l == 31)),
                             reads=[w1b, posB], writes=[bankbuf[7]] if l == 0 else [], acc=[] if l == 0 else [bankbuf[7]])
                    P.op('dve', lambda e, c=c, b1_sb=b1_sb: e.tensor_tensor(out=bias_sb[:, c:c + 1], in0=banks[7][:, 0:1],
                                                                           in1=b1_sb[:, c:c + 1], op=ALU.add),
                         reads=[bankbuf[7], cbuf], writes=[biasB] if c == 0 else [], acc=[] if c == 0 else [biasB])
                cnt = 0
                for c in range(2):
                    for gp in range(2):
                        bk = cnt % 2
                        cnt += 1
                        ov = banks[bk][:, 0:2 * NCMP].rearrange("p (g n) -> p g n", g=2)
                        for l in range(32):
                            P.op('pe', lambda e, c=c, gp=gp, l=l, ov=ov: e.matmul(
                                ov, lhsT=w1[0:64, l, c * 128:(c + 1) * 128],
                                rhs=KC[0:64, 2 * gp:2 * gp + 2, l:l + 16 * (NCMP - 1) + 1:16], start=(l == 0), stop=(l == 31)),
                                reads=[w1b, KCb], writes=[bankbuf[bk]] if l == 0 else [], acc=[] if l == 0 else [bankbuf[bk]])
                        P.op('act', lambda e, c=c, gp=gp, ov=ov: e.activation(
                            out=hT[:, c, 2 * gp:2 * gp + 2, 0:NCMP], in_=ov, func=AF.Silu, bias=bias_sb[:, c:c + 1]),
                            reads=[bankbuf[bk], biasB, hTb], acc=[hTb])
                if kind == 'k':
                    for gp in range(2):
                        ov = banks[2 + gp][0:64, 0:2 * NCMP].rearrange("p (g n) -> p g n", g=2)
                        for c in range(2):
                            P.op('pe', lambda e, c=c, gp=gp, ov=ov: e.matmul(
                                ov, lhsT=w2[:, c, :], rhs=hT[:, c, 2 * gp:2 * gp + 2, 0:NCMP], start=(c == 0), stop=(c == 1)),
                                reads=[w2B, hTb], writes=[bankbuf[2 + gp]] if c == 0 else [], acc=[] if c == 0 else [bankbuf[2 + gp]])
                        cp('dve', KcT[0:64, 2 * gp:2 * gp + 2, 0:NCMP], ov, reads=[bankbuf[2 + gp], kcb], acc=[kcb])
                else:
                    cnt = 0
                    for g in range(4):
                        for NT in range(NNT):
                            bk = 2 + cnt % 2
                            cnt += 1
                            for c in range(2):
                                P.op('pe', lambda e, c=c, g=g, NT=NT, bk=bk: e.matmul(
                                    banks[bk][:, 0:64], lhsT=hT[:, c, g, NT * 128:(NT + 1) * 128], rhs=w2[:, c, :],
                                    start=(c == 0), stop=(c == 1)),
                                    reads=[w2B, hTb], writes=[bankbuf[bk]] if c == 0 else [], acc=[] if c == 0 else [bankbuf[bk]])
                            cp('dve', Vc[:, g, NT, 0:64], banks[bk][:, 0:64], reads=[bankbuf[bk], vcb], acc=[vcb])
            if 'dbg_kc' in debug and b == 0:
                dk = nc.dram_tensor('dbg_kc', [64, 4, NP], BF, kind="ExternalOutput").ap()
                dv = nc.dram_tensor('dbg_vc', [128, 4, NNT, 65], BF, kind="ExternalOutput").ap()
                P.dma('sp', next_cc(), dk, KcT[0:64], reads=[kcb])
                P.dma('sp', next_cc(), dv, Vc, reads=[vcb])
            P.barrier()
            A.release(mc)
            if stop_after <= 3:
                A.release(m0)
                return

            Kaug = A.alloc(S, BF)
            Kw = A.alloc(S, BF)
            Vs = A.alloc(NKT * 65, BF).rearrange("p (t c) -> p t c", t=NKT)
            Vw = A.alloc(NKT * 65, BF).rearrange("p (t c) -> p t c", t=NKT)
            EBs = A.alloc(4 * US, BF).rearrange("p (h u) -> p h u", h=4)
            EBw = A.alloc(4 * UW, BF).rearrange("p (h u) -> p h u", h=4)
            grpB = Buf('grp')
            Qaug = [A.alloc(4 * 512, BF).rearrange("p (h t) -> p h t", h=4) for _ in range(2)]
            Qb = [Buf('Qa0'), Buf('Qa1')]
            Gt = [A.alloc(4 * 48, F32).rearrange("p (s c) -> p s c", s=4) for _ in range(2)]
            SZ = [A.alloc(2 * 512, BF).rearrange("p (f t) -> p f t", f=2) for _ in range(2)]
            qinB = [Buf('qin0'), Buf('qin1')]
            EBc = [A.alloc(4 * 512, BF).rearrange("p (h t) -> p h t", h=4) for _ in range(2)]
            EBcB = [Buf('ebc0'), Buf('ebc1')]
            Pt = [A.alloc(512, BF) for _ in range(4)]
            Pb = [Buf(f'P{i}') for i in range(4)]
            o_acc = A.alloc(1024, F32)
            o4 = o_acc.rearrange("p (s h d) -> p s h d", s=4, h=4)
            oaB = Buf('oacc')
            imp = A.alloc(256, F32).rearrange("p (s j) -> p s j", s=4)
            impB = Buf('imp')
            tmpA = [A.alloc(256, F32).rearrange("p (s j) -> p s j", s=4) for _ in range(2)]
            tmpB = [Buf('tmpA0'), Buf('tmpA1')]
            m8 = A.alloc(32, F32).rearrange("p (s k) -> p s k", s=4)
            thr = A.alloc(4, F32)
            selB = Buf('sel')
            selpad = A.alloc(512, BF).rearrange("p (s c) -> p s c", s=4)
            rs = [A.alloc(4, F32) for _ in range(2)]
            cf = [A.alloc(4, F32) for _ in range(2)]
            rsB = [Buf('rs0'), Buf('rs1')]
            ozst = [A.alloc(512, BF) for _ in range(2)]
            ozB = [Buf('oz0'), Buf('oz1')]
            P.op('dve', lambda e: e.memset(selpad, 0.0), writes=[selB])
            P.op('pool', lambda e: e.memset(Vs, 1.0), writes=[grpB])
            P.op('pool', lambda e: e.memset(Vw, 1.0), acc=[grpB])
            Tb = banks[6][:, 0:256].bitcast(BF)
            T32 = banks[6]
            qcnt = 0
            ocnt = 0
            pcnt = 0
            ecnt = 0
            mulc = 0
            ozc = 0
            def do_qt(g, QT):
                nonlocal qcnt, ocnt, pcnt, ecnt, mulc, ozc
                qs = qcnt % 2
                qcnt += 1
                Qa = Qaug[qs]
                G = Gt[qs]
                P.dma('sp', chan(f'q{qs}'), Qa[0:64], qT_d[b, g * 256:(g + 1) * 256, QT * 512:(QT + 1) * 512]
                      .rearrange("(h d) t -> d h t", d=64), reads=[scr[b]], writes=[Qb[qs]])
                P.dma('sp', chan(f'q{qs}'), G, gate_d[b, QT * 512:(QT + 1) * 512, :].rearrange("(s p) c -> p s c", p=128),
                      reads=[scr[b]], writes=[qinB[qs]])
                P.dma('sp', chan(f'q{qs}'), SZ[qs], szT_d[b, g * 256:(g + 1) * 256, QT * 512:(QT + 1) * 512]
                      .rearrange("(f p) t -> p f t", p=128), reads=[scr[b]], acc=[qinB[qs]])
                items = []
                nts = [NT for NT in range(NNT) if 2048 * NT + 31 <= 512 * QT + 511]
                for h in range(4):
                    for i, NT in enumerate(nts):
                        items.append(dict(br=0, h=h, kt=NT, first=(i == 0), last=(i == len(nts) - 1)))
                def slc_sw(h):
                    kts = list(range(0, 4 * QT + 4))
                    for i, KT in enumerate(kts):
                        items.append(dict(br=1, h=h, kt=KT, first=(i == 0), last=(i == len(kts) - 1)))
                    kts = list(range(max(0, 4 * QT - 4), 4 * QT + 4))
                    for i, KT in enumerate(kts):
                        items.append(dict(br=2, h=h, kt=KT, first=(i == 0), last=(i == len(kts) - 1)))
                for h in range(4):
                    slc_sw(h)
                ebc_slot = {}
                for NT in nts:
                    es_ = ecnt % 2
                    ecnt += 1
                    c0 = 512 * QT - 2048 * NT - 31
                    P.dma('sp', chan(f'ebc{es_}'), EBc[es_], bass.AP(reps_d.tensor, g * 4 * 128 * LS + (c0 - MINV),
                                                                      [[LS - 16, 128], [128 * LS, 4], [1, 512]]),
                          reads=[repbuf], writes=[EBcB[es_]])
                    ebc_slot[NT] = es_

                def front(it):
                    nonlocal pcnt, mulc, ocnt
                    si = (0, 1, 7)[pcnt % 3]
                    pi = pcnt % 4
                    pcnt += 1
                    it['pi'] = pi
                    h, KT, br = it['h'], it['kt'], it['br']
                    hg = g * 4 + h
                    if it['first']:
                        it['ob'] = ocnt % 2
                        ocnt += 1
                    else:
                        it['ob'] = it['prev']['ob']
                    sb = banks[si]
                    if br == 0:
                        P.op('pe', lambda e: e.matmul(sb[:, :], lhsT=KcT[0:64, g, KT * 128:(KT + 1) * 128],
                                                     rhs=Qa[0:64, h, :], start=True, stop=True),
                             reads=[kcb, Qb[qs]], writes=[bankbuf[si]])
                    elif br == 1:
                        P.op('pe', lambda e: e.matmul(sb[:, :], lhsT=Kaug[:, KT * 128:(KT + 1) * 128],
                                                     rhs=Qa[:, h, :], start=True, stop=True),
                             reads=[grpB, Qb[qs]], writes=[bankbuf[si]])
                    else:
                        P.op('pe', lambda e: e.matmul(sb[:, :], lhsT=Kw[0:64, KT * 128:(KT + 1) * 128],
                                                     rhs=Qa[0:64, h, :], start=True, stop=True),
                             reads=[grpB, Qb[qs]], writes=[bankbuf[si]])
                    off = 512 * QT - 128 * KT
                    far = (br == 1 and off >= 1024)
                    if far:
                        P.op('act', lambda e: e.activation(out=Pt[pi], in_=sb[:, :], func=AF.Exp, bias=b31_bc[:, hg:hg + 1]),
                             reads=[bankbuf[si], cbuf], writes=[Pb[pi]])
                    else:
                        P.op('act', lambda e: e.activation(out=Pt[pi], in_=sb[:, :], func=AF.Exp),
                             reads=[bankbuf[si]], writes=[Pb[pi]])
                        if br == 0:
                            ebt = EBc[ebc_slot[KT]][:, h, :]
                            rd = [EBcB[ebc_slot[KT]]]
                        elif br == 1:
                            ebt = EBs[:, h, off + 384:off + 384 + 512]
                            rd = [grpB]
                        else:
                            ebt = EBw[:, h, off + 384:off + 384 + 512]
                            rd = [grpB]
                        eng = 'dve' if mulc % 2 == 0 else 'pool'
                        mulc += 1
                        P.op(eng, lambda e: e.tensor_tensor(out=Pt[pi], in0=Pt[pi], in1=ebt, op=ALU.mult),
                             reads=rd + [Pb[pi]], acc=[Pb[pi]])

                def back(it):
                    h, KT, br, pi, ob = it['h'], it['kt'], it['br'], it['pi'], it['ob']
                    hg = g * 4 + h
                    Ob = banks[2 + ob]
                    Ib = banks[4 + ob]
                    if it['first']:
                        P.op('pe', lambda e: e.matmul(Ob[:, 0:260], lhsT=zero_b[0:1, 0:128], rhs=zero_b[0:1, 0:260],
                                                     start=True, stop=False, skip_group_check=True),
                             reads=[cbuf], writes=[bankbuf[2 + ob]])
                        if br == 0:
                            P.op('pe', lambda e: e.matmul(Ib[:, 0:256], lhsT=zero_b[0:1, 0:128], rhs=zero_b[0:1, 0:256],
                                                         start=True, stop=False, skip_group_check=True),
                                 reads=[cbuf], writes=[bankbuf[4 + ob]])
                    for sub in range(4):
                        if br == 1 and KT > 4 * QT + sub:
                            continue
                        if br == 2 and (KT > 4 * QT + sub or KT < 4 * QT + sub - 4):
                            continue
                        if br == 0:
                            rhs = Vc[:, g, KT, :]
                            rd = [vcb]
                        elif br == 1:
                            rhs = Vs[:, KT, :]
                            rd = [grpB]
                        else:
                            rhs = Vw[:, KT, :]
                            rd = [grpB]
                        P.op('pe', lambda e, sub=sub, rhs=rhs: e.matmul(
                            Ob[:, sub * 65:(sub + 1) * 65], lhsT=Pt[pi][:, sub * 128:(sub + 1) * 128], rhs=rhs,
                            start=False, stop=False, skip_group_check=True),
                            reads=[Pb[pi]] + rd, acc=[bankbuf[2 + ob]])
                        if br == 0:
                            P.op('pe', lambda e, sub=sub: e.matmul(
                                Ib[:, sub * 64:(sub + 1) * 64], lhsT=Pt[pi][:, sub * 128:(sub + 1) * 128], rhs=OV_sb[:, KT, :],
                                start=False, stop=False, skip_group_check=True),
                                reads=[Pb[pi], cbuf], acc=[bankbuf[4 + ob]])
                    if not it['last']:
                        return
                    r = ob
                    O3 = Ob[:, 0:260].rearrange("p (s c) -> p s c", s=4)
                    P.op('dve', lambda e: e.tensor_scalar(rs[r], O3[:, :, 64], 1e-30, None, ALU.max),
                         reads=[bankbuf[2 + ob]], writes=[rsB[r]])
                    P.op('dve', lambda e: e.reciprocal(rs[r], rs[r]), reads=[rsB[r]], acc=[rsB[r]])
                    gi = g * 12 + h * 3 + br
                    P.op('dve', lambda e: e.tensor_tensor(out=cf[r], in0=rs[r], in1=G[:, :, gi], op=ALU.mult),
                         reads=[qinB[qs], rsB[r]], acc=[rsB[r]])
                    cfb = cf[r].unsqueeze(2).to_broadcast([128, 4, 64])
                    if br == 0:
                        P.op('dve', lambda e: e.tensor_tensor(out=o4[:, :, h, :], in0=O3[:, :, 0:64], in1=cfb, op=ALU.mult),
                             reads=[bankbuf[2 + ob], rsB[r]], writes=[oaB] if h == 0 else [], acc=[] if h == 0 else [oaB])
                        I3 = Ib[:, 0:256].rearrange("p (s j) -> p s j", s=4)
                        rsb = rs[r].unsqueeze(2).to_broadcast([128, 4, 64])
                        if h == 0:
                            P.op('dve', lambda e: e.tensor_tensor(out=imp, in0=I3, in1=rsb, op=ALU.mult),
                                 reads=[bankbuf[4 + ob], rsB[r]], writes=[impB])
                        else:
                            P.op('dve', lambda e: e.tensor_tensor(out=tmpA[r], in0=I3, in1=rsb, op=ALU.mult),
                                 reads=[bankbuf[4 + ob], rsB[r]], writes=[tmpB[r]])
                            P.op('pool', lambda e: e.tensor_tensor(out=imp, in0=imp, in1=tmpA[r], op=ALU.add),
                                 reads=[tmpB[r], impB], acc=[impB])
                    else:
                        P.op('dve', lambda e: e.tensor_tensor(out=tmpA[r], in0=O3[:, :, 0:64], in1=cfb, op=ALU.mult),
                             reads=[bankbuf[2 + ob], rsB[r]], writes=[tmpB[r]])
                        P.op('pool', lambda e: e.tensor_tensor(out=o4[:, :, h, :], in0=o4[:, :, h, :], in1=tmpA[r], op=ALU.add),
                             reads=[tmpB[r], oaB], acc=[oaB])

                def selection():
                    P.op('dve', lambda e: e.tensor_tensor(out=imp, in0=imp, in1=ADD_sb[:, 4 * QT:4 * QT + 4, :], op=ALU.add),
                         reads=[cbuf, impB], acc=[impB])
                    for sub in range(4):
                        P.op('dve', lambda e, sub=sub: e.max(out=m8[:, sub, :], in_=imp[:, sub, :]), reads=[impB], acc=[selB])
                    P.op('dve', lambda e: e.tensor_scalar(thr, m8[:, :, 7], 0.0, None, ALU.max), reads=[selB], acc=[selB])
                    for sub in range(4):
                        P.op('dve', lambda e, sub=sub: e.tensor_scalar(selpad[:, sub, 64:128], imp[:, sub, :],
                                                                       thr[:, sub:sub + 1], NEG, ALU.is_lt, ALU.mult),
                             reads=[impB, selB], acc=[selB])
                    for sub in range(4):
                        P.op('pe', lambda e, sub=sub: e.transpose(out=Tb[:, sub * 128:(sub + 1) * 128], in_=selpad[:, sub, :],
                                                                 identity=ident_b),
                             reads=[selB, cbuf], writes=[bankbuf[6]] if sub == 0 else [], acc=[] if sub == 0 else [bankbuf[6]])
                    P.op('act', lambda e: e.activation(out=Qa[64:128, :, :], in_=Tb[64:128, 0:512].unsqueeze(1).to_broadcast([64, 4, 512]),
                                                      func=AF.Copy),
                         reads=[bankbuf[6], Qb[qs]], acc=[Qb[qs]])

                for idx, it in enumerate(items):
                    it['prev'] = items[idx - 1] if idx > 0 else None
                n_cmp_items = 4 * len(nts)
                LA = 2

                def run_pipeline(its):
                    n = len(its)
                    for idx in range(n + LA):
                        if idx < n:
                            front(its[idx])
                        j = idx - LA
                        if j >= 0:
                            back(its[j])

                run_pipeline(items[:n_cmp_items])
                selection()
                run_pipeline(items[n_cmp_items:])
                for fc in range(2):
                    for sub in range(4):
                        P.op('pe', lambda e, sub=sub, fc=fc: e.transpose(
                            out=T32[:, sub * 128:(sub + 1) * 128], in_=o_acc[:, sub * 256 + fc * 128:sub * 256 + (fc + 1) * 128],
                            identity=ident_f),
                            reads=[oaB, cbuf], writes=[bankbuf[6]] if sub == 0 else [], acc=[] if sub == 0 else [bankbuf[6]])
                    zs = ozc % 2
                    ozc += 1
                    P.op('dve', lambda e, fc=fc, zs=zs: e.tensor_tensor(out=ozst[zs], in0=T32[:, :], in1=SZ[qs][:, fc, :], op=ALU.mult),
                         reads=[bankbuf[6], qinB[qs]], writes=[ozB[zs]])
                    P.dma('pool', chan(f'oz{zs}'), ozT_d[b, (g * 2 + fc) * 128:(g * 2 + fc + 1) * 128, QT * 512:(QT + 1) * 512],
                          ozst[zs], reads=[ozB[zs]], acc=[scr2[b]])

            for g in range(4):
                P.dma('sp', chan('g0'), Kaug[0:64, :], ksT_d[b, g * 64:(g + 1) * 64, :], reads=[scr[b]], writes=[grpB])
                P.dma('sp', chan('g0'), Kaug[64:128, :], cd['Ec'], acc=[grpB])
                P.dma('sp', chan('g0'), Kw[0:64, :], kwT_d[b, g * 64:(g + 1) * 64, :], reads=[scr[b]], acc=[grpB])
                P.dma('sp', chan('g0'), Vs[:, :, 0:64], vs_d[b, :, g * 64:(g + 1) * 64].rearrange("(t p) c -> p t c", p=128),
                      reads=[scr[b]], acc=[grpB])
                P.dma('sp', chan('g0'), Vw[:, :, 0:64], vw_d[b, :, g * 64:(g + 1) * 64].rearrange("(t p) c -> p t c", p=128),
                      reads=[scr[b]], acc=[grpB])
                P.dma('sp', chan('g0'), EBs, bass.AP(reps_d.tensor, g * 4 * 128 * LS + (-384 - MINV),
                                                      [[LS - 1, 128], [128 * LS, 4], [1, US]]), reads=[repbuf], acc=[grpB])
                P.dma('sp', chan('g0'), EBw, bass.AP(repw_d.tensor, g * 4 * 128 * LW + 127,
                                                      [[LW - 1, 128], [128 * LW, 4], [1, UW]]), reads=[repbuf], acc=[grpB])
                for QT in range(NQT):
                    do_qt(g, QT)
            P.barrier()
            A.release(m0)


        def do_ssd(b):
            m0 = A.mark()
            NBLK = S // 512
            XP = [A.alloc(24 * 515, BF).rearrange("p (c t) -> p c t", c=24)] * 2
            _xpb = Buf('XP0')
            XPb = [_xpb, _xpb]
            XC = A.alloc(24 * 512, BF).rearrange("p (c t) -> p c t", c=24)
            XCb = Buf('XC')
            acc32 = [A.alloc(512, F32) for _ in range(2)]
            accB = [Buf('acc0'), Buf('acc1')]
            XS = A.alloc(4 * 2048, BF).rearrange("p (s f) -> p s f", s=4)
            XSb = Buf('XS')
            BT = A.alloc(4 * 512, BF).rearrange("p (s f) -> p s f", s=4)
            BTb = Buf('BT')
            DT = [A.alloc(4 * 32, F32).rearrange("p (s h) -> p s h", s=4)] * 2
            SZS = [A.alloc(4 * 2048, BF).rearrange("p (s f) -> p s f", s=4)] * 2
            _inb = Buf('ssdin0')
            inB = [_inb, _inb]
            st32 = A.alloc(2048, F32)
            stbf = A.alloc(2048, BF)
            stB = Buf('state')
            stbB = Buf('statebf')
            dtA = A.alloc(32, F32)
            cstot = A.alloc(64, F32)
            ecs = A.alloc(64, F32)
            toend = A.alloc(32, F32)
            smB = Buf('ssm_small')
            xD = A.alloc(2048, BF)
            xw = A.alloc(2048, BF)
            xdB = Buf('xD')
            CBm = A.alloc(128, F32)
            CBb = Buf('CBm')
            LH = [A.alloc(128, F32) for _ in range(4)]
            LHb = [Buf(f'LH{i}') for i in range(4)]
            ED = [A.alloc(512, F32) for _ in range(2)]
            EDb = [Buf('ED0'), Buf('ED1')]
            WT = [A.alloc(128, BF) for _ in range(8)]
            WTb = [Buf(f'WT{i}') for i in range(8)]
            Ysb = A.alloc(512, F32)
            YsB = Buf('Ysb')
            ytmp = A.alloc(512, F32)
            ytB = Buf('ytmp')
            yfull = A.alloc(2048, F32)
            yfB = Buf('yfull')
            stmp = A.alloc(512, F32)
            stmpB = Buf('stmp')
            ss4 = A.alloc(16, F32)
            ssB = Buf('ss4')
            sqj = A.alloc(512, BF)
            sqjB = Buf('sqj')
            hn = A.alloc(2048, BF)
            hnB = Buf('hn')
            ynst = [A.alloc(16 * 512, BF).rearrange("p (c t) -> p c t", c=16)] * 2
            _ynb = Buf('yn0')
            ynB = [_ynb, _ynb]
            P.op('dve', lambda e: e.memset(st32, 0.0), writes=[stB])
            P.op('dve', lambda e: e.memset(stbf, 0.0), writes=[stbB])
            P.op('dve', lambda e: e.memset(XP[0][:, :, 0:3], 0.0), writes=[XPb[0]])
            Tb8 = banks[7][:, :].bitcast(BF)

            def do_chunk(blk, sub, sl, ysl):
                dts = DT[sl][:, sub, :]
                P.op('dve', lambda e: e.tensor_tensor(out=dtA, in0=dts, in1=A_bc, op=ALU.mult), reads=[inB[sl], cbuf], writes=[smB])
                P.op('pe', lambda e: e.matmul(banks[0][:, 0:32], lhsT=U_sb, rhs=dtA, start=True, stop=True),
                     reads=[smB, cbuf], writes=[bankbuf[0]])
                P.op('pe', lambda e: e.matmul(banks[0][:, 32:64], lhsT=ONES_sb, rhs=dtA, start=True, stop=True),
                     reads=[smB, cbuf], acc=[bankbuf[0]])
                P.op('dve', lambda e: e.tensor_copy(out=cstot, in_=banks[0][:, 0:64]), reads=[bankbuf[0], smB], acc=[smB])
                P.op('act', lambda e: e.activation(out=ecs, in_=cstot, func=AF.Exp), reads=[smB], acc=[smB])
                P.op('dve', lambda e: e.tensor_tensor(out=toend, in0=cstot[:, 32:64], in1=cstot[:, 0:32], op=ALU.subtract),
                     reads=[smB], acc=[smB])
                P.op('act', lambda e: e.activation(out=toend, in_=toend, func=AF.Exp), reads=[smB], acc=[smB])
                P.op('dve', lambda e: e.tensor_tensor(out=toend, in0=toend, in1=dts, op=ALU.mult), reads=[smB, inB[sl]], acc=[smB])
                xs3 = XS[:, sub, :].rearrange("p (h d) -> p h d", h=32)
                P.op('pool', lambda e: e.tensor_tensor(out=xD.rearrange("p (h d) -> p h d", h=32), in0=xs3,
                                                       in1=dsk_bc[:, 0:32].unsqueeze(2).to_broadcast([128, 32, 64]), op=ALU.mult),
                     reads=[XSb, cbuf], writes=[xdB])
                P.op('pool', lambda e: e.tensor_tensor(out=xw.rearrange("p (h d) -> p h d", h=32), in0=xs3,
                                                       in1=toend[:, 0:32].unsqueeze(2).to_broadcast([128, 32, 64]), op=ALU.mult),
                     reads=[XSb, smB, xdB], acc=[xdB])
                tsl = slice(sub * 128, (sub + 1) * 128)
                for g in range(4):
                    do_group(blk, sub, sl, g, tsl, dts)
                P.op('dve', lambda e: e.tensor_tensor(out=yfull, in0=yfull, in1=SZS[sl][:, sub, :], op=ALU.mult),
                     reads=[yfB, inB[sl]], acc=[yfB])
                for g in range(4):
                    P.op('act', lambda e, g=g: e.activation(out=sqj, in_=yfull[:, g * 512:(g + 1) * 512], func=AF.Square,
                                                           accum_out=ss4[:, g:g + 1]),
                         reads=[yfB], writes=[sqjB, ssB] if g == 0 else [sqjB], acc=[ssB] if g else [])
                P.op('dve', lambda e: e.tensor_scalar(ss4[:, 4:8], ss4[:, 0:4], 1.0 / 512, EPS, ALU.mult, ALU.add), reads=[ssB], acc=[ssB])
                P.op('act', lambda e: e.activation(out=ss4[:, 8:12], in_=ss4[:, 4:8], func=AF.Sqrt), reads=[ssB], acc=[ssB])
                P.op('dve', lambda e: e.reciprocal(ss4[:, 12:16], ss4[:, 8:12]), reads=[ssB], acc=[ssB])
                P.op('dve', lambda e: e.tensor_tensor(out=hn.rearrange("p (g f) -> p g f", g=4),
                                                      in0=yfull.rearrange("p (g f) -> p g f", g=4),
                                                      in1=ss4[:, 12:16].unsqueeze(2).to_broadcast([128, 4, 512]), op=ALU.mult),
                     reads=[yfB, ssB], writes=[hnB])
                for half in range(2):
                    for c8 in range(8):
                        c = half * 8 + c8
                        P.op('pe', lambda e, c=c, c8=c8: e.transpose(out=Tb8[:, c8 * 128:(c8 + 1) * 128], in_=hn[:, c * 128:(c + 1) * 128],
                                                                     identity=ident_b),
                             reads=[hnB, cbuf], writes=[bankbuf[7]] if c8 == 0 else [], acc=[] if c8 == 0 else [bankbuf[7]])
                    dst = ynst[ysl][:, half * 8:(half + 1) * 8, tsl]
                    src = Tb8.rearrange("p (c t) -> p c t", c=8)
                    first = (sub == 0 and half == 0)
                    if half == 0:
                        P.op('act', lambda e, dst=dst, src=src: e.activation(out=dst, in_=src, func=AF.Copy),
                             reads=[bankbuf[7]] + ([] if first else [ynB[ysl]]), writes=[ynB[ysl]] if first else [], acc=[] if first else [ynB[ysl]])
                    else:
                        P.op('dve', lambda e, dst=dst, src=src: e.tensor_copy(out=dst, in_=src),
                             reads=[bankbuf[7], ynB[ysl]], acc=[ynB[ysl]])

            def do_group(blk, sub, sl, g, tsl, dts):
                P.op('pe', lambda e: e.matmul(banks[1][:, 0:128], lhsT=XC[:, 16 + g, tsl], rhs=XC[:, 20 + g, tsl], start=True, stop=True),
                     reads=[XCb], writes=[bankbuf[1]])
                P.op('dve', lambda e: e.tensor_tensor(out=CBm, in0=banks[1][:, 0:128], in1=U_sb, op=ALU.mult),
                     reads=[bankbuf[1], cbuf], writes=[CBb])
                for h4 in range(2):
                    bk = 2 + h4
                    for q in range(4):
                        hh = h4 * 4 + q
                        h = g * 8 + hh
                        P.op('pool', lambda e, q=q, h=h: e.tensor_scalar(LH[q], LST_sb, dtA[:, h:h + 1], 1.0, ALU.mult, ALU.mult),
                             reads=[smB, cbuf], writes=[LHb[q]])
                        P.op('pe', lambda e, q=q, bk=bk: e.matmul(banks[bk][:, q * 128:(q + 1) * 128], lhsT=LH[q], rhs=U_sb, start=True, stop=True),
                             reads=[LHb[q], cbuf], writes=[bankbuf[bk]] if q == 0 else [], acc=[] if q == 0 else [bankbuf[bk]])
                    P.op('act', lambda e, bk=bk, h4=h4: e.activation(out=ED[h4], in_=banks[bk][:, :], func=AF.Exp),
                         reads=[bankbuf[bk]], writes=[EDb[h4]])
                    for q in range(4):
                        hh = h4 * 4 + q
                        h = g * 8 + hh
                        P.op('dve', lambda e, q=q, hh=hh, h=h, h4=h4: e.scalar_tensor_tensor(
                            out=WT[hh], in0=CBm, scalar=dts[:, h:h + 1], in1=ED[h4][:, q * 128:(q + 1) * 128], op0=ALU.mult, op1=ALU.mult),
                            reads=[CBb, EDb[h4], inB[sl]], writes=[WTb[hh]])
                gs = slice(g * 512, (g + 1) * 512)
                P.op('pe', lambda e: e.matmul(banks[4][:, :], lhsT=ident_b, rhs=xD[:, gs], start=True, stop=False, skip_group_check=True),
                     reads=[xdB, cbuf], writes=[bankbuf[4]])
                for hh in range(8):
                    h = g * 8 + hh
                    P.op('pe', lambda e, hh=hh, h=h: e.matmul(banks[4][:, hh * 64:(hh + 1) * 64], lhsT=WT[hh],
                                                             rhs=XS[:, sub, h * 64:(h + 1) * 64], start=False, stop=False, skip_group_check=True),
                         reads=[WTb[hh], XSb], acc=[bankbuf[4]])
                P.op('pe', lambda e: e.matmul(banks[5][:, :], lhsT=XC[:, 20 + g, tsl], rhs=stbf[:, gs], start=True, stop=True),
                     reads=[XCb, stbB], writes=[bankbuf[5]])
                P.op('act', lambda e: e.activation(out=Ysb, in_=banks[4][:, :], func=AF.Copy), reads=[bankbuf[4]], writes=[YsB])
                P.op('dve', lambda e: e.tensor_tensor(out=ytmp.rearrange("p (h d) -> p h d", h=8),
                                                      in0=banks[5][:, :].rearrange("p (h d) -> p h d", h=8),
                                                      in1=ecs[:, g * 8:(g + 1) * 8].unsqueeze(2).to_broadcast([128, 8, 64]), op=ALU.mult),
                     reads=[bankbuf[5], smB], writes=[ytB])
                P.op('pool', lambda e: e.tensor_tensor(out=yfull[:, gs], in0=ytmp, in1=Ysb, op=ALU.add),
                     reads=[ytB, YsB], writes=[yfB] if g == 0 else [], acc=[] if g == 0 else [yfB])
                P.op('pe', lambda e: e.matmul(banks[6][:, :], lhsT=BT[:, sub, g * 128:(g + 1) * 128], rhs=xw[:, gs], start=True, stop=True),
                     reads=[BTb, xdB], writes=[bankbuf[6]])
                P.op('pool', lambda e: e.tensor_tensor(out=stmp.rearrange("p (h d) -> p h d", h=8),
                                                       in0=st32[:, gs].rearrange("p (h d) -> p h d", h=8),
                                                       in1=ecs[:, 32 + g * 8:32 + (g + 1) * 8].unsqueeze(2).to_broadcast([128, 8, 64]), op=ALU.mult),
                     reads=[stB, smB], writes=[stmpB])
                P.op('dve', lambda e: e.tensor_tensor(out=st32[:, gs], in0=stmp, in1=banks[6][:, :], op=ALU.add),
                     reads=[stmpB, bankbuf[6], stB], acc=[stB])
                P.op('act', lambda e: e.activation(out=stbf[:, gs], in_=st32[:, gs], func=AF.Copy), reads=[stB, stbB], acc=[stbB])

            def do_block(blk):
                sl = 0
                xp = XP[sl]
                P.dma('sp', chan(f'xp{sl}'), xp[:, :, 3:515], xbcT_d[b, :, blk * 512:(blk + 1) * 512].rearrange("(c p) t -> p c t", p=128),
                      reads=[scr[b], XPb[sl]], acc=[XPb[sl]])
                P.dma('sp', chan(f'si{sl}'), DT[sl], dt_d[b, blk * 512:(blk + 1) * 512, :].rearrange("(s p) h -> p s h", p=128),
                      reads=[scr[b]], writes=[inB[sl]])
                P.dma('sp', chan(f'si{sl}'), SZS[sl], szs_d[b, blk * 512:(blk + 1) * 512, :].rearrange("(s p) f -> p s f", p=128),
                      reads=[scr[b], inB[sl]], acc=[inB[sl]])
                for c in range(24):
                    a = c % 2
                    ac = acc32[a]
                    P.op('dve', lambda e, c=c, ac=ac: e.tensor_scalar(ac, xp[:, c, 0:512], convw_sb[:, c, 0:1], convb_sb[:, c:c + 1],
                                                                     ALU.mult, ALU.add),
                         reads=[XPb[sl], cbuf], writes=[accB[a]])
                    for k in range(1, 4):
                        P.op('dve', lambda e, c=c, k=k, ac=ac: e.scalar_tensor_tensor(
                            out=ac, in0=xp[:, c, k:k + 512], scalar=convw_sb[:, c, k:k + 1], in1=ac, op0=ALU.mult, op1=ALU.add),
                            reads=[XPb[sl], cbuf, accB[a]], acc=[accB[a]])
                    P.op('act', lambda e, c=c, ac=ac: e.activation(out=XC[:, c, :], in_=ac, func=AF.Silu),
                         reads=[accB[a]] + ([XCb] if c else []), writes=[XCb] if c == 0 else [], acc=[] if c == 0 else [XCb])
                P.op('pool', lambda e: e.tensor_copy(out=xp[:, :, 0:3], in_=xp[:, :, 512:515]), reads=[XPb[sl]], acc=[XPb[sl]])
                for sub in range(4):
                    tsl = slice(sub * 128, (sub + 1) * 128)
                    for half in range(2):
                        for c8 in range(8):
                            c = half * 8 + c8
                            P.op('pe', lambda e, c=c, c8=c8, tsl=tsl: e.transpose(out=Tb8[:, c8 * 128:(c8 + 1) * 128], in_=XC[:, c, tsl],
                                                                               identity=ident_b),
                                 reads=[XCb, cbuf], writes=[bankbuf[7]] if c8 == 0 else [], acc=[] if c8 == 0 else [bankbuf[7]])
                        first = (sub == 0 and half == 0)
                        eng = 'act' if half == 0 else 'dve'
                        cp(eng, XS[:, sub, half * 1024:(half + 1) * 1024], Tb8[:, :], reads=[bankbuf[7]] + ([] if first else [XSb]),
                           writes=[XSb] if first else [], acc=[] if first else [XSb])
                    for gq in range(4):
                        P.op('pe', lambda e, gq=gq, tsl=tsl: e.transpose(out=Tb8[:, gq * 128:(gq + 1) * 128], in_=XC[:, 16 + gq, tsl],
                                                                       identity=ident_b),
                             reads=[XCb, cbuf], writes=[bankbuf[7]] if gq == 0 else [], acc=[] if gq == 0 else [bankbuf[7]])
                    cp('act', BT[:, sub, :], Tb8[:, 0:512], reads=[bankbuf[7]] + ([] if sub == 0 else [BTb]),
                       writes=[BTb] if sub == 0 else [], acc=[] if sub == 0 else [BTb])
                for sub in range(4):
                    do_chunk(blk, sub, sl, sl)
                P.dma('pool', chan(f'yn{sl}'), ynT_d[b, :, blk * 512:(blk + 1) * 512].rearrange("(c p) t -> p c t", p=128),
                      ynst[sl], reads=[ynB[sl]], acc=[scr2[b]])

            for blk in range(NBLK):
                do_block(blk)
            P.barrier()
            A.release(m0)

        def do_out(b):
            m0 = A.mark()
            Wn = A.alloc(8 * 1024, BF).rearrange("p (k c) -> p k c", k=8)
            Ws = A.alloc(16 * 1024, BF).rearrange("p (k c) -> p k c", k=16)
            Wo = A.alloc(8 * 1024, BF).rearrange("p (k c) -> p k c", k=8)
            load_out_weights(Wn, Ws, Wo)
            OZ = [A.alloc(8 * 512, BF).rearrange("p (k t) -> p k t", k=8)] * 2
            YN = [A.alloc(16 * 512, BF).rearrange("p (k t) -> p k t", k=16)] * 2
            GT = [A.alloc(16 * 512, BF).rearrange("p (k t) -> p k t", k=16)] * 2
            X = [A.alloc(4 * 1024, F32).rearrange("p (s d) -> p s d", s=4)] * 2
            _ld = Buf('ld0')
            ldB = [_ld, _ld]
            mT = A.alloc(8 * 512, BF).rearrange("p (k t) -> p k t", k=8)
            mTb = Buf('mT')
            t1 = [A.alloc(512, F32) for _ in range(2)]
            t2 = [A.alloc(512, F32) for _ in range(2)]
            t1B = [Buf('t1_0'), Buf('t1_1')]
            t2B = [Buf('t2_0'), Buf('t2_1')]
            R = [A.alloc(1024, F32) for _ in range(2)]
            RB = [Buf('R0'), Buf('R1')]
            OS = [A.alloc(1024, F32) for _ in range(2)]
            OSB = [Buf('OS0'), Buf('OS1')]
            sq = A.alloc(1024, BF)
            sqB = Buf('sq')
            s4 = [A.alloc(4, F32) for _ in range(2)]
            s4B = [Buf('s4_0'), Buf('s4_1')]

            def do_blk(blk):
                sl = 0
                tsl = slice(blk * 512, (blk + 1) * 512)
                P.dma('sp', chan(f'oa{sl}'), OZ[sl], ozT_d[b, :, tsl].rearrange("(k p) t -> p k t", p=128), reads=[scr2[b]], writes=[ldB[sl]])
                P.dma('sp', chan(f'ob{sl}'), YN[sl], ynT_d[b, :, tsl].rearrange("(k p) t -> p k t", p=128), reads=[scr2[b], ldB[sl]], acc=[ldB[sl]])
                P.dma('sp', chan(f'oc{sl}'), GT[sl], gT_d[b, :, tsl].rearrange("(k p) t -> p k t", p=128), reads=[scr[b], ldB[sl]], acc=[ldB[sl]])
                P.dma('sp', chan(f'od{sl}'), X[sl], x_d[b, tsl, :].rearrange("(s p) d -> p s d", p=128), reads=[ldB[sl]], acc=[ldB[sl]])
                for fo in range(8):
                    a = fo % 2
                    fsl = slice(fo * 128, (fo + 1) * 128)
                    for k in range(8):
                        P.op('pe', lambda e, k=k, fsl=fsl, a=a: e.matmul(banks[a][:, :], lhsT=Wn[:, k, fsl], rhs=OZ[sl][:, k, :],
                                                                        start=(k == 0), stop=(k == 7)),
                             reads=[wobuf, ldB[sl]], writes=[bankbuf[a]] if k == 0 else [], acc=[] if k == 0 else [bankbuf[a]])
                    for k in range(16):
                        P.op('pe', lambda e, k=k, fsl=fsl, a=a: e.matmul(banks[2 + a][:, :], lhsT=Ws[:, k, fsl], rhs=YN[sl][:, k, :],
                                                                        start=(k == 0), stop=(k == 15)),
                             reads=[wobuf, ldB[sl]], writes=[bankbuf[2 + a]] if k == 0 else [], acc=[] if k == 0 else [bankbuf[2 + a]])
                    P.op('dve', lambda e, fo=fo, a=a: e.tensor_tensor(out=t1[a], in0=banks[a][:, :], in1=GT[sl][:, fo, :], op=ALU.mult),
                         reads=[bankbuf[a], ldB[sl]], writes=[t1B[a]])
                    P.op('dve', lambda e, fo=fo, a=a: e.tensor_tensor(out=t2[a], in0=banks[2 + a][:, :], in1=GT[sl][:, 8 + fo, :], op=ALU.mult),
                         reads=[bankbuf[2 + a], ldB[sl]], writes=[t2B[a]])
                    P.op('pool', lambda e, fo=fo, a=a: e.tensor_tensor(out=mT[:, fo, :], in0=t1[a], in1=t2[a], op=ALU.add),
                         reads=[t1B[a], t2B[a]] + ([mTb] if fo else []), writes=[mTb] if fo == 0 else [], acc=[] if fo == 0 else [mTb])
                for sub in range(4):
                    r = sub % 2
                    for cb in range(2):
                        bk = 4 + (sub * 2 + cb) % 4
                        csl = slice(cb * 512, (cb + 1) * 512)
                        for k in range(8):
                            P.op('pe', lambda e, k=k, bk=bk, csl=csl, sub=sub: e.matmul(
                                banks[bk][:, :], lhsT=mT[:, k, sub * 128:(sub + 1) * 128], rhs=Wo[:, k, csl], start=(k == 0), stop=(k == 7)),
                                reads=[mTb, wobuf], writes=[bankbuf[bk]] if k == 0 else [], acc=[] if k == 0 else [bankbuf[bk]])
                        P.op('dve', lambda e, bk=bk, csl=csl, sub=sub, r=r: e.tensor_tensor(out=R[r][:, csl], in0=banks[bk][:, :],
                                                                                         in1=X[sl][:, sub, csl], op=ALU.add),
                             reads=[bankbuf[bk], ldB[sl]] + ([RB[r]] if cb else []), writes=[RB[r]] if cb == 0 else [], acc=[] if cb == 0 else [RB[r]])
                    P.op('act', lambda e, r=r: e.activation(out=sq, in_=R[r], func=AF.Square, accum_out=s4[r][:, 0:1]),
                         reads=[RB[r]], writes=[sqB, s4B[r]])
                    P.op('dve', lambda e, r=r: e.tensor_scalar(s4[r][:, 1:2], s4[r][:, 0:1], 1.0 / D, EPS, ALU.mult, ALU.add),
                         reads=[s4B[r]], acc=[s4B[r]])
                    P.op('act', lambda e, r=r: e.activation(out=s4[r][:, 2:3], in_=s4[r][:, 1:2], func=AF.Sqrt), reads=[s4B[r]], acc=[s4B[r]])
                    P.op('dve', lambda e, r=r: e.reciprocal(s4[r][:, 3:4], s4[r][:, 2:3]), reads=[s4B[r]], acc=[s4B[r]])
                    P.op('dve', lambda e, r=r: e.scalar_tensor_tensor(out=OS[r], in0=R[r], scalar=s4[r][:, 3:4], in1=fnw_bc,
                                                                      op0=ALU.mult, op1=ALU.mult),
                         reads=[RB[r], s4B[r], cbuf], writes=[OSB[r]])
                    if 'out_nostore' not in debug:
                        P.dma('sp', chan(f'out{r}'),
                              out_d[b, blk * 512 + sub * 128:blk * 512 + (sub + 1) * 128, :], OS[r],
                              reads=[OSB[r]], acc=[outbuf])

            for blk in range(S // 512):
                do_blk(blk)
            P.barrier()
            A.release(m0)

        outbuf = Buf('out')

        scr = [Buf(f'scratch{i}') for i in range(NB)]
        scr2 = [Buf(f'scratch2_{i}') for i in range(NB)]
        for b in range(NB):
            seq_mark = A.mark()
            xnT = A.alloc(8 * S, BF).rearrange("p (k s) -> p k s", k=8)
            xnb = [Buf(f'xnT{m}') for m in range(NTT)]
            m1 = A.mark()
            xt = [A.alloc(1024, F32) for _ in range(3)]
            xtb = [Buf(f'xt{i}') for i in range(3)]
            xch = [P.new_chan() for _ in range(3)] if b == 0 else xch
            junk = A.alloc(1024, BF)
            junkb = Buf('junk')
            st4 = [A.alloc(4, F32) for _ in range(3)]
            st4b = [Buf('st4') for _ in range(3)]
            for m in range(NTT):
                sl = m % 3
                P.dma('sp', xch[sl], xt[sl], x_d[b, m * 128:(m + 1) * 128, :], writes=[xtb[sl]])
                s4 = st4[sl]
                P.op('act', lambda e, sl=sl, s4=s4: e.activation(out=junk, in_=xt[sl], func=AF.Square, accum_out=s4[:, 0:1]),
                     reads=[xtb[sl]], writes=[junkb, st4b[sl]])
                P.op('dve', lambda e, s4=s4: e.tensor_scalar(s4[:, 1:2], s4[:, 0:1], 1.0 / D, EPS, ALU.mult, ALU.add),
                     reads=[st4b[sl]], acc=[st4b[sl]])
                P.op('act', lambda e, s4=s4: e.activation(out=s4[:, 2:3], in_=s4[:, 1:2], func=AF.Sqrt),
                     reads=[st4b[sl]], acc=[st4b[sl]])
                P.op('dve', lambda e, s4=s4: e.reciprocal(s4[:, 3:4], s4[:, 2:3]), reads=[st4b[sl]], acc=[st4b[sl]])
                P.op('act', lambda e, sl=sl, s4=s4: e.activation(out=xt[sl], in_=xt[sl], func=AF.Copy, scale=s4[:, 3:4]),
                     reads=[st4b[sl], xtb[sl]], acc=[xtb[sl]])
                for half in range(2):
                    bk = (2 * m + half) % 4
                    for kk in range(4):
                        kc = half * 4 + kk
                        P.op('pe', lambda e, bk=bk, kk=kk, kc=kc, sl=sl: e.transpose(
                            out=banks[bk][:, kk * 128:(kk + 1) * 128], in_=xt[sl][:, kc * 128:(kc + 1) * 128], identity=ident_f),
                            reads=[xtb[sl], cbuf], writes=[bankbuf[bk]] if kk == 0 else [], acc=[] if kk == 0 else [bankbuf[bk]])
                    eng = 'dve' if half == 0 else 'act'
                    dst = xnT[:, half * 4:half * 4 + 4, m * 128:(m + 1) * 128]
                    src = banks[bk][:, :].rearrange("p (k t) -> p k t", k=4)
                    if eng == 'dve':
                        P.op('dve', lambda e, dst=dst, src=src: e.tensor_copy(out=dst, in_=src),
                             reads=[bankbuf[bk]], acc=[xnb[m]])
                    else:
                        P.op('act', lambda e, dst=dst, src=src: e.activation(out=dst, in_=src, func=AF.Copy),
                             reads=[bankbuf[bk]], acc=[xnb[m]])
            if 'dbg_xnT' in debug and b == 0:
                dbg_x = nc.dram_tensor('dbg_xnT', [128, 8, S], BF, kind="ExternalOutput").ap()
                P.dma('sp', next_cc(), dbg_x, xnT, reads=xnb)
            if stop_after <= 1:
                break

            W32 = [A.alloc(8 * 512, F32).rearrange("p (k c) -> p k c", k=8) for _ in range(2)]
            W32b = [Buf('w32_0'), Buf('w32_1')]
            Wb = [A.alloc(8 * 512, BF).rearrange("p (k c) -> p k c", k=8) for _ in range(2)]
            Wbb = [Buf('wb0'), Buf('wb1')]
            stF = [A.alloc(S, BF) for _ in range(3)]
            stFb = [Buf(f'stF{i}') for i in range(3)]
            stT = [A.alloc(4 * 512, F32) for _ in range(2)]
            stTb = [Buf(f'stT{i}') for i in range(2)]
            tmp32 = A.alloc(64, F32)
            tmp32b = Buf('tmp32')
            if b == 0:
                wch = [P.new_chan(), P.new_chan()]
                fch = [P.new_chan() for _ in range(3)]
                tch = [P.new_chan() for _ in range(2)]
            scrbuf = scr[b]
            w_v = w_in_d.rearrange("(k p) c -> p k c", p=128)
            blocks = []

            def fm(c0, n, func, scale, dest, r0):
                for cc in range(0, n, 512):
                    nn = min(512, n - cc)
                    blocks.append(('F', c0 + cc, nn, func, scale, dest, r0 + cc))

            def tm(c0, n, func, dest, dc0, dt):
                for cc in range(0, n, 512):
                    nn = min(512, n - cc)
                    blocks.append(('T', c0 + cc, nn, func, 1.0, dest, dc0 + cc, dt))

            fm(C_Q, 1024, 'copy', 0.125, qT_d, 0)
            fm(C_KC, 256, 'copy', 1.0, kcT_d, 0)
            fm(C_VC, 256, 'copy', 1.0, vcT_d, 0)
            fm(C_KS, 256, 'copy', 1.0, ksT_d, 0)
            tm(C_VS, 256, 'copy', vs_d, 0, BF)
            fm(C_KW, 256, 'copy', 1.0, kwT_d, 0)
            tm(C_VW, 256, 'copy', vw_d, 0, BF)
            tm(C_GATE, 48, 'sigmoid', gate_d, 0, F32)
            fm(C_ZN, 1024, 'silu', 1.0, szT_d, 0)
            tm(C_ZS, 2048, 'silu', szs_d, 0, BF)
            fm(C_XBC, 3072, 'copy', 1.0, xbcT_d, 0)
            tm(C_DT, 32, 'softplus', dt_d, 0, F32)
            fm(C_MG, 2048, 'sigmoid', 1.0, gT_d, 0)

            bkc = [0]
            fcnt = [0]
            tcnt = [0]
            evc = [0]

            def evac(func, scale, dst, src, reads, writes=(), acc=()):
                if func == 'copy':
                    evc[0] += 1
                    if evc[0] % 2 == 0:
                        P.op('dve', lambda e: e.tensor_scalar(dst, src, float(scale), None, ALU.mult),
                             reads=reads, writes=writes, acc=acc)
                    else:
                        P.op('act', lambda e: e.activation(out=dst, in_=src, func=AF.Copy, scale=float(scale)),
                             reads=reads, writes=writes, acc=acc)
                elif func == 'silu':
                    P.op('act', lambda e: e.activation(out=dst, in_=src, func=AF.Silu), reads=reads, writes=writes, acc=acc)
                elif func == 'sigmoid':
                    P.op('act', lambda e: e.activation(out=dst, in_=src, func=AF.Sigmoid), reads=reads, writes=writes, acc=acc)
                else:
                    raise ValueError(func)

            for bi, blk in enumerate(blocks):
                kind, c0, n = blk[0], blk[1], blk[2]
                sl = bi % 2
                P.dma('sp', wch[sl], W32[sl][:, :, 0:n], w_v[:, :, c0:c0 + n], writes=[W32b[sl]])
                P.op('dve', lambda e, sl=sl, n=n: e.tensor_tensor(
                    out=Wb[sl][:, :, 0:n], in0=W32[sl][:, :, 0:n],
                    in1=normw_sb[:, 0:8].unsqueeze(2).to_broadcast([128, 8, n]), op=ALU.mult),
                    reads=[W32b[sl], cbuf], writes=[Wbb[sl]])
                if kind == 'F':
                    _, _, _, func, scale, dest, r0 = blk
                    for j in range(n // 128):
                        fs = fcnt[0] % 3
                        fcnt[0] += 1
                        for tt in range(S // 512):
                            bk = bkc[0] % 4
                            bkc[0] += 1
                            for kc in range(8):
                                P.op('pe', lambda e, bk=bk, sl=sl, j=j, kc=kc, tt=tt: e.matmul(
                                    banks[bk][:, :], lhsT=Wb[sl][:, kc, j * 128:(j + 1) * 128],
                                    rhs=xnT[:, kc, tt * 512:(tt + 1) * 512], start=(kc == 0), stop=(kc == 7)),
                                    reads=[Wbb[sl]] + xnb[tt * 4:tt * 4 + 4], writes=[bankbuf[bk]] if kc == 0 else [],
                                    acc=[] if kc == 0 else [bankbuf[bk]])
                            evac(func, scale, stF[fs][:, tt * 512:(tt + 1) * 512], banks[bk][:, :], [bankbuf[bk]],
                                 writes=[stFb[fs]] if tt == 0 else [], acc=[] if tt == 0 else [stFb[fs]])
                        P.dma('pool', fch[fs], dest[b, r0 + j * 128:r0 + (j + 1) * 128, :], stF[fs][:, 0:S],
                              reads=[stFb[fs]], acc=[scrbuf])
                else:
                    _, _, _, func, scale, dest, dc0, ddt = blk
                    for mg in range(NTT // 4):
                        ts_ = tcnt[0] % 2
                        tcnt[0] += 1
                        stv = stT[ts_] if ddt == F32 else stT[ts_].bitcast(BF)
                        stv = stv[:, 0:4 * n].rearrange("p (m c) -> p m c", m=4)
                        for mm in range(4):
                            m = mg * 4 + mm
                            bk = bkc[0] % 4
                            bkc[0] += 1
                            for kc in range(8):
                                P.op('pe', lambda e, bk=bk, sl=sl, kc=kc, m=m, n=n: e.matmul(
                                    banks[bk][:, 0:n], lhsT=xnT[:, kc, m * 128:(m + 1) * 128],
                                    rhs=Wb[sl][:, kc, 0:n], start=(kc == 0), stop=(kc == 7)),
                                    reads=[Wbb[sl], xnb[m]], writes=[bankbuf[bk]] if kc == 0 else [],
                                    acc=[] if kc == 0 else [bankbuf[bk]])
                            wr = dict(writes=[stTb[ts_]] if mm == 0 else [], acc=[] if mm == 0 else [stTb[ts_]])
                            if func == 'softplus':
                                P.op('dve', lambda e, bk=bk, n=n: e.tensor_tensor(out=tmp32[:, 0:n], in0=banks[bk][:, 0:n],
                                                                                   in1=dtb_bc[:, 0:n], op=ALU.add),
                                     reads=[bankbuf[bk], cbuf], writes=[tmp32b])
                                P.op('act', lambda e, n=n: e.activation(out=tmp32[:, 0:n], in_=tmp32[:, 0:n], func=AF.Exp),
                                     reads=[tmp32b], acc=[tmp32b])
                                P.op('act', lambda e, n=n, stv=stv, mm=mm: e.activation(out=stv[:, mm, :], in_=tmp32[:, 0:n],
                                                                                          func=AF.Ln, bias=1.0),
                                     reads=[tmp32b], **wr)
                            else:
                                evac(func, scale, stv[:, mm, :], banks[bk][:, 0:n], [bankbuf[bk]], **wr)
                        P.dma('pool', tch[ts_],
                              dest[b, mg * 512:(mg + 1) * 512, dc0:dc0 + n].rearrange("(m p) c -> p m c", p=128),
                              stv, reads=[stTb[ts_]], acc=[scrbuf])
            A.release(seq_mark)
            P.barrier()
            if stop_after <= 2:
                continue
            do_nsa(b)
            if stop_after <= 4:
                continue
            do_ssd(b)
            if stop_after <= 5:
                continue
            do_out(b)

        P.barrier()
        P.op('sp', lambda e: e.nop(), reads=[], writes=[])
        P.op('act', lambda e: e.activation(out=tinyc[:, 0:1], in_=tinyc[:, 1:2], func=AF.Copy), reads=[cbuf], writes=[])

        @block.tensor
        def _(e):
            P.emit('pe', e, esems, csems)

        @block.scalar
        def _(e):
            P.emit('act', e, esems, csems)

        @block.vector
        def _(e):
            P.emit('dve', e, esems, csems)

        @block.gpsimd
        def _(e):
            P.emit('pool', e, esems, csems)

        @block.sync
        def _(e):
            P.emit('sp', e, esems, csems)

    return nc, hc


def phase_cmp_nsa(*a):
    raise NotImplementedError


def phase_ssd(*a):
    raise NotImplementedError


def phase_out(*a):
    raise NotImplementedError


def make_in_maps(inputs, hc, NB, ncores):
    f = lambda a: np.ascontiguousarray(np.asarray(a, dtype=np.float32))
    x = f(inputs['x'])
    shared = {
        'w_in': f(inputs['w_in'][0]),
        'norm_w': f(inputs['norm_w'][0].reshape(8, 128).T),
        'posT_k': f(inputs['cmp_pos_k'][0].T),
        'posT_v': f(inputs['cmp_pos_v'][0].T),
        'w1_k': f(inputs['cmp_k_w1'][0].reshape(32, 64, 256).transpose(1, 0, 2)),
        'w1_v': f(inputs['cmp_v_w1'][0].reshape(32, 64, 256).transpose(1, 0, 2)),
        'b1_k': f(inputs['cmp_k_b1'][0].reshape(2, 128).T),
        'b1_v': f(inputs['cmp_v_b1'][0].reshape(2, 128).T),
        'w2_k': f(inputs['cmp_k_w2'][0].reshape(2, 128, 64).transpose(1, 0, 2)),
        'w2_v': f(inputs['cmp_v_w2'][0].reshape(2, 128, 64).transpose(1, 0, 2)),
        'conv_w': f(inputs['conv_w'][0].reshape(4, 24, 128).transpose(2, 1, 0)),
        'conv_b': f(inputs['conv_b'][0].reshape(24, 128).T),
        'dt_bias': f(inputs['dt_bias'][0].reshape(1, 32)),
        'a_log': f(inputs['a_log'][0].reshape(1, 32)),
        'd_skip': f(inputs['d_skip'][0].reshape(1, 32)),
        'ssm_norm_w': f(inputs['ssm_norm_w'][0].reshape(16, 128).T),
        'w_out_nsa': f(inputs['w_out_nsa'][0]),
        'w_out_ssm': f(inputs['w_out_ssm'][0]),
        'w_out': f(inputs['w_out'][0]),
        'rel_bias': f(inputs['rel_bias']),
        'final_norm_w': f(inputs['final_norm_w'].reshape(1, D)),
    }
    for nme in CONST_NAMES:
        shared['c_' + nme] = np.ascontiguousarray(hc[nme])
    maps = []
    for c in range(ncores):
        m = dict(shared)
        m['x'] = np.ascontiguousarray(x[c * NB:(c + 1) * NB])
        maps.append(m)
    return maps


_CACHE = {}


def kernel(**inputs):
    x = np.asarray(inputs['x'])
    B, S, _ = x.shape
    ncores = 8
    NB = B // ncores
    key = (S, NB)
    if key not in _CACHE:
        _CACHE[key] = build(S, NB)
    nc, hc = _CACHE[key]
    maps = make_in_maps(inputs, hc, NB, ncores)
    res = run_bass_kernel_spmd(nc, maps, core_ids=list(range(ncores)))
    out = np.concatenate([np.asarray(r['out']) for r in res.results], axis=0)
    return out.astype(np.float32)
```
